# Optimizing a Trainium2 kernel written in Bass

```python
import jax, jax.numpy as jnp
from jax import lax
import numpy as np

D_MODEL = 1024
BATCH = 4
SEQ = 4096
DEPTH = 2
DEC_BATCH = 32
DEC_SEQ = 64
PAST_LEN = 2048

CHUNK = 64
EPS = 1e-6
MLA_HEADS = 4
Q_LORA = 384
KV_LORA = 256
QK_NOPE = 128
QK_ROPE = 64
V_HEAD = 128
MLA_WIDTH = MLA_HEADS * V_HEAD
MLA_SCALE = (QK_NOPE + QK_ROPE) ** -0.5
ROPE_BASE = 10000.0
Q_BLOCK = 128
HGRN_HEADS = 4
HGRN_EXPAND = 64
HGRN_HEAD_DIM = 64
HGRN_KEY = HGRN_HEADS * HGRN_EXPAND
HGRN_WIDTH = HGRN_HEADS * HGRN_HEAD_DIM
CMLP_GROUPS = 4
CMLP_GROUP_DIM = 64
CMLP_WIDTH = CMLP_GROUPS * CMLP_GROUP_DIM
CMLP_CHUNK = 128
MIX_WIDTH = MLA_WIDTH + HGRN_WIDTH + CMLP_WIDTH
D_FF = 2816
IN_SIZES = (Q_LORA, KV_LORA, QK_ROPE, HGRN_KEY, HGRN_KEY, HGRN_WIDTH, HGRN_WIDTH, CMLP_WIDTH, CMLP_WIDTH)
IN_WIDTH = Q_LORA + KV_LORA + QK_ROPE + 2 * HGRN_KEY + 2 * HGRN_WIDTH + 2 * CMLP_WIDTH

kernel_name = 'hybrid_stream_mla_hgrn2_gmlp_step'


def _in_offsets():
    out, acc = [], 0
    for s in IN_SIZES[:-1]:
        acc += s
        out.append(acc)
    return out


def _rmsnorm(x, g):
    x32 = x.astype(jnp.float32)
    y = x32 * lax.rsqrt(jnp.mean(x32 * x32, axis=-1, keepdims=True) + EPS)
    return (y * g.astype(jnp.float32)).astype(x.dtype)


def _swiglu(x, w_gate, w_up, w_down):
    return (jax.nn.silu(x @ w_gate) * (x @ w_up)) @ w_down


def _rope_tables(pos, dim):
    inv_freq = ROPE_BASE ** (-jnp.arange(0, dim, 2, dtype=jnp.float32) / dim)
    ang = pos.astype(jnp.float32)[:, None] * inv_freq[None, :]
    return jnp.cos(ang), jnp.sin(ang)


def _apply_rope(x, cos, sin):
    x32 = x.astype(jnp.float32)
    half = x.shape[-1] // 2
    x1, x2 = x32[..., :half], x32[..., half:]
    return jnp.concatenate([x1 * cos - x2 * sin, x2 * cos + x1 * sin], axis=-1).astype(x.dtype)


def _mla_attend(q_lat, q_pe, c_all, kpe_all, q_pos, k_pos):
    s = (jnp.einsum('bqhc,bkc->bhqk', q_lat, c_all)
         + jnp.einsum('bqhr,bkr->bhqk', q_pe, kpe_all)).astype(jnp.float32) * MLA_SCALE
    visible = (k_pos[None, :] // CHUNK) <= (q_pos[:, None] // CHUNK)
    s = jnp.where(visible[None, None], s, -jnp.inf)
    p = jax.nn.softmax(s, axis=-1).astype(c_all.dtype)
    return jnp.einsum('bhqk,bkc->bqhc', p, c_all)


def _mla(q_a, kv_a, kpe_raw, pos, qa_g, w_qb, kva_g, w_uk, w_uv, c_past, kpe_past):
    B, T, _ = q_a.shape
    q = (_rmsnorm(q_a, qa_g) @ w_qb).reshape(B, T, MLA_HEADS, QK_NOPE + QK_ROPE)
    q_nope, q_pe = q[..., :QK_NOPE], q[..., QK_NOPE:]
    cos, sin = _rope_tables(pos, QK_ROPE)
    q_pe = _apply_rope(q_pe, cos[:, None, :], sin[:, None, :])
    c_new = _rmsnorm(kv_a, kva_g)
    kpe_new = _apply_rope(kpe_raw, cos, sin)
    q_lat = jnp.einsum('bthn,chn->bthc', q_nope, w_uk)
    if c_past is None:
        c_all, kpe_all = c_new, kpe_new
    else:
        c_all = jnp.concatenate([c_past.astype(c_new.dtype), c_new], axis=1)
        kpe_all = jnp.concatenate([kpe_past.astype(kpe_new.dtype), kpe_new], axis=1)
    k_pos = jnp.arange(c_all.shape[1], dtype=jnp.int32)
    if T > Q_BLOCK and T % Q_BLOCK == 0:
        nb = T // Q_BLOCK
        blk = lambda a: jnp.moveaxis(a.reshape((B, nb, Q_BLOCK) + a.shape[2:]), 1, 0)
        o_lat = lax.map(lambda args: _mla_attend(args[0], args[1], c_all, kpe_all, args[2], k_pos),
                        (blk(q_lat), blk(q_pe), pos.reshape(nb, Q_BLOCK)))
        o_lat = jnp.moveaxis(o_lat, 0, 1).reshape(B, T, MLA_HEADS, KV_LORA)
    else:
        o_lat = _mla_attend(q_lat, q_pe, c_all, kpe_all, pos, k_pos)
    o = jnp.einsum('bthc,chv->bthv', o_lat, w_uv).reshape(B, T, MLA_WIDTH)
    return o, c_new, kpe_new


def _hgrn2(hq, hf, hi, hg, lb, out_g, S0):
    B, T, _ = hq.shape
    f32 = jnp.float32
    zf = hf.astype(f32)
    logf = jnp.logaddexp(jnp.log(lb), jnp.log1p(-lb) + jax.nn.log_sigmoid(zf))
    k = (1.0 - lb) * jax.nn.sigmoid(-zf)
    q = hq.astype(f32).reshape(B, T, HGRN_HEADS, HGRN_EXPAND)
    k = k.reshape(B, T, HGRN_HEADS, HGRN_EXPAND)
    logf = logf.reshape(B, T, HGRN_HEADS, HGRN_EXPAND)
    v = hi.astype(f32).reshape(B, T, HGRN_HEADS, HGRN_HEAD_DIM)
    L = min(CHUNK, T)
    n = T // L
    to_chunks = lambda a: jnp.moveaxis(a.reshape((B, n, L) + a.shape[2:]), 1, 0)
    causal = jnp.tril(jnp.ones((L, L), dtype=bool))[None, :, :, None, None]

    def step(S, inp):
        qc, kc, vc, lc = inp
        b = jnp.cumsum(lc, axis=1)
        o_inter = jnp.einsum('bthe,bhed->bthd', qc * jnp.exp(b), S)
        diff = b[:, :, None] - b[:, None, :]
        decay = jnp.where(causal, jnp.exp(jnp.where(causal, diff, 0.0)), 0.0)
        A = jnp.einsum('bthe,btshe,bshe->bhts', qc, decay, kc)
        o_intra = jnp.einsum('bhts,bshd->bthd', A, vc)
        b_last = b[:, -1]
        S_new = (jnp.exp(b_last)[..., None] * S
                 + jnp.einsum('bshe,bshd->bhed', kc * jnp.exp(b_last[:, None] - b), vc))
        return S_new, o_inter + o_intra

    if S0 is None:
        S0 = jnp.zeros((B, HGRN_HEADS, HGRN_EXPAND, HGRN_HEAD_DIM), f32)
    else:
        S0 = S0.astype(f32)
    S_fin, o = lax.scan(step, S0, (to_chunks(q), to_chunks(k), to_chunks(v), to_chunks(logf)))
    o = jnp.moveaxis(o, 0, 1).reshape(B, T, HGRN_HEADS, HGRN_HEAD_DIM)
    gate = jax.nn.silu(hg.astype(f32).reshape(B, T, HGRN_HEADS, HGRN_HEAD_DIM))
    o = _rmsnorm(o, out_g.reshape(HGRN_HEADS, HGRN_HEAD_DIM)) * gate
    return o.reshape(B, T, HGRN_WIDTH).astype(hq.dtype), S_fin


def _chunk_mlp(cu, cv, v_g, w_s, b_s):
    B, T, _ = cu.shape
    u = jax.nn.gelu(cu)
    v = _rmsnorm(jax.nn.gelu(cv).reshape(B, T, CMLP_GROUPS, CMLP_GROUP_DIM),
                 v_g.reshape(CMLP_GROUPS, CMLP_GROUP_DIM))
    L = min(CMLP_CHUNK, T)
    n = T // L
    w = w_s[:, :L, :L] * jnp.tril(jnp.ones((L, L), w_s.dtype))
    mixed = (jnp.einsum('gts,bnsgd->bntgd', w, v.reshape(B, n, L, CMLP_GROUPS, CMLP_GROUP_DIM))
             + jnp.transpose(b_s[:, :L])[None, None, :, :, None])
    out = u * mixed.reshape(B, T, CMLP_WIDTH).astype(u.dtype)
    return out, v.reshape(B, T, CMLP_WIDTH)


def _trunk(x, pos, cache_ckv, cache_kpe, state_S, p):
    offsets = _in_offsets()
    lb_all = jnp.cumsum(jax.nn.softmax(p['hgrn_lb_logits'].astype(jnp.float32), axis=0), axis=0)
    lb_all = lb_all - lb_all[0:1]
    ckv_rows, kpe_rows, states, v_rows = [], [], [], []
    for l in range(DEPTH):
        ng = p['norm_g'][l]
        h = _rmsnorm(x, ng[0])
        x = x + 0.5 * _rmsnorm(_swiglu(h, p['ffn_w_gate'][l, 0], p['ffn_w_up'][l, 0], p['ffn_w_down'][l, 0]), ng[1])
        h = _rmsnorm(x, ng[2])
        z = h @ p['w_in'][l]
        q_a, kv_a, kpe_raw, hq, hf, hi, hg, cu, cv = jnp.split(z, offsets, axis=-1)
        o_a, c_new, kpe_new = _mla(q_a, kv_a, kpe_raw, pos, p['mla_qa_g'][l], p['mla_wqb'][l],
                                   p['mla_kva_g'][l], p['mla_w_uk'][l], p['mla_w_uv'][l],
                                   None if cache_ckv is None else cache_ckv[l],
                                   None if cache_kpe is None else cache_kpe[l])
        o_b, S_new = _hgrn2(hq, hf, hi, hg, lb_all[l], p['hgrn_out_g'][l],
                            None if state_S is None else state_S[l])
        o_c, v_new = _chunk_mlp(cu, cv, p['cmlp_v_g'][l], p['cmlp_w_s'][l], p['cmlp_b_s'][l])
        mix = jnp.concatenate([o_a, o_b.astype(o_a.dtype), o_c.astype(o_a.dtype)], axis=-1) @ p['w_out'][l]
        x = x + _rmsnorm(mix, ng[3])
        h = _rmsnorm(x, ng[4])
        x = x + 0.5 * _rmsnorm(_swiglu(h, p['ffn_w_gate'][l, 1], p['ffn_w_up'][l, 1], p['ffn_w_down'][l, 1]), ng[5])
        ckv_rows.append(c_new)
        kpe_rows.append(kpe_new)
        states.append(S_new)
        v_rows.append(v_new)
    return x, jnp.stack(ckv_rows), jnp.stack(kpe_rows), jnp.stack(states), jnp.stack(v_rows)


def setup_inputs(seed: int = 0) -> dict:
    key = jax.random.key(seed)
    ks = jax.random.split(key, 24)
    f32 = jnp.float32
    nrm = lambda k, shape, scale: jax.random.normal(k, shape, f32) * scale
    gain = lambda k, shape: 1.0 + 0.02 * jax.random.normal(k, shape, f32)
    return {
        'x_prompt': nrm(ks[0], (BATCH, SEQ, D_MODEL), 1.0),
        'x_sample': nrm(ks[1], (DEC_BATCH, DEC_SEQ, D_MODEL), 1.0),
        'cache_mla_ckv': nrm(ks[2], (DEPTH, DEC_BATCH, PAST_LEN, KV_LORA), 1.0),
        'cache_mla_kpe': nrm(ks[3], (DEPTH, DEC_BATCH, PAST_LEN, QK_ROPE), 1.0),
        'state_hgrn': nrm(ks[4], (DEPTH, DEC_BATCH, HGRN_HEADS, HGRN_EXPAND, HGRN_HEAD_DIM), 0.5),
        'norm_g': gain(ks[5], (DEPTH, 6, D_MODEL)),
        'ffn_w_gate': nrm(ks[6], (DEPTH, 2, D_MODEL, D_FF), D_MODEL ** -0.5),
        'ffn_w_up': nrm(ks[7], (DEPTH, 2, D_MODEL, D_FF), D_MODEL ** -0.5),
        'ffn_w_down': nrm(ks[8], (DEPTH, 2, D_FF, D_MODEL), D_FF ** -0.5),
        'w_in': nrm(ks[9], (DEPTH, D_MODEL, IN_WIDTH), D_MODEL ** -0.5),
        'w_out': nrm(ks[10], (DEPTH, MIX_WIDTH, D_MODEL), MIX_WIDTH ** -0.5),
        'mla_qa_g': gain(ks[11], (DEPTH, Q_LORA)),
        'mla_wqb': nrm(ks[12], (DEPTH, Q_LORA, MLA_HEADS * (QK_NOPE + QK_ROPE)), Q_LORA ** -0.5),
        'mla_kva_g': gain(ks[13], (DEPTH, KV_LORA)),
        'mla_w_uk': nrm(ks[14], (DEPTH, KV_LORA, MLA_HEADS, QK_NOPE), KV_LORA ** -0.5),
        'mla_w_uv': nrm(ks[15], (DEPTH, KV_LORA, MLA_HEADS, V_HEAD), KV_LORA ** -0.5),
        'hgrn_lb_logits': nrm(ks[16], (DEPTH, HGRN_KEY), 0.5),
        'hgrn_out_g': gain(ks[17], (DEPTH, HGRN_WIDTH)),
        'cmlp_v_g': gain(ks[18], (DEPTH, CMLP_WIDTH)),
        'cmlp_w_s': nrm(ks[19], (DEPTH, CMLP_GROUPS, CMLP_CHUNK, CMLP_CHUNK), CMLP_CHUNK ** -0.5),
        'cmlp_b_s': 1.0 + 0.1 * jax.random.normal(ks[20], (DEPTH, CMLP_GROUPS, CMLP_CHUNK), f32),
    }


def reference(x_prompt, x_sample, cache_mla_ckv, cache_mla_kpe, state_hgrn, norm_g, ffn_w_gate,
              ffn_w_up, ffn_w_down, w_in, w_out, mla_qa_g, mla_wqb, mla_kva_g, mla_w_uk, mla_w_uv,
              hgrn_lb_logits, hgrn_out_g, cmlp_v_g, cmlp_w_s, cmlp_b_s):
    p = dict(norm_g=norm_g, ffn_w_gate=ffn_w_gate, ffn_w_up=ffn_w_up, ffn_w_down=ffn_w_down,
             w_in=w_in, w_out=w_out, mla_qa_g=mla_qa_g, mla_wqb=mla_wqb, mla_kva_g=mla_kva_g,
             mla_w_uk=mla_w_uk, mla_w_uv=mla_w_uv, hgrn_lb_logits=hgrn_lb_logits,
             hgrn_out_g=hgrn_out_g, cmlp_v_g=cmlp_v_g, cmlp_w_s=cmlp_w_s, cmlp_b_s=cmlp_b_s)
    pos_p = jnp.arange(x_prompt.shape[1], dtype=jnp.int32)
    y_prompt, ckv_p, kpe_p, hgrn_p, _ = _trunk(x_prompt, pos_p, None, None, None, p)
    past = cache_mla_ckv.shape[2]
    pos_s = past + jnp.arange(x_sample.shape[1], dtype=jnp.int32)
    y_sample, ckv_s, kpe_s, hgrn_s, cmlp_v_s = _trunk(x_sample, pos_s, cache_mla_ckv, cache_mla_kpe, state_hgrn, p)
    return (y_prompt, y_sample, ckv_p, kpe_p, hgrn_p, ckv_s, kpe_s, hgrn_s, cmlp_v_s)
```

```python
import numpy as np
import concourse.bass as bass
import concourse.mybir as mybir
from concourse.bass_utils import run_bass_kernel_spmd

F32 = mybir.dt.float32
BF16 = mybir.dt.bfloat16
AF = mybir.ActivationFunctionType
ALU = mybir.AluOpType
AX = mybir.AxisListType

D = 1024
DFF = 2816
NJ = DFF // 128
SEQ = 4096
NPT = 8
TN = 512
SN = 256
NSEQ = 4
PAST = 2048
EPS = 1e-6
MLA_SCALE = 192 ** -0.5
DEPTH = 2
EPOCH = 8000
NDS = 40
O_QA, O_KV, O_KPE, O_HQ, O_HF, O_HI, O_HG, O_CU, O_CV = 0, 384, 640, 704, 960, 1216, 1472, 1728, 1984
R_NG, R_QAG, R_KVG, R_HOG, R_LB = 0, 96, 102, 106, 114

class V:
    def __init__(self, buf, ap):
        self.buf = buf
        self.ap = ap

    def __getitem__(self, idx):
        return V(self.buf, self.ap[idx])

    def bitcast(self, dt):
        return V(self.buf, self.ap.bitcast(dt))

    def rearrange(self, s, **kw):
        return V(self.buf, self.ap.rearrange(s, **kw))

    def bcast(self, shape):
        return V(self.buf, self.ap.broadcast_to(shape))

class Buf:
    def __init__(self, t, track=True):
        self.t = t
        self.w = None
        self.r = {}
        self.track = track
        self.psum = False

    def __getitem__(self, idx):
        return V(self, self.t[idx])

class Pool:
    def __init__(self, bufs, name):
        self.free_list = list(bufs)
        self.name = name
        self.n = len(bufs)
        self.low = len(bufs)

    def alloc(self):
        assert self.free_list, "pool %s exhausted" % self.name
        b = self.free_list.pop(0)
        self.low = min(self.low, len(self.free_list))
        return b

    def free(self, *bs):
        for b in bs:
            assert b not in self.free_list
            self.free_list.append(b)

class Ctx:
    ENG = ("pe", "act", "dve", "pool", "sp")

    def __init__(self, nc):
        self.nc = nc
        self.prog = {e: [] for e in self.ENG}
        self.cnt = {e: 0 for e in self.ENG}
        self.esem = {}
        self.seen = {e: {} for e in self.ENG}
        self.dsems = [nc.alloc_semaphore("dq%d" % i) for i in range(NDS)]
        self.dcnt = [0] * NDS
        self.dnext = 0
        self.dry = False
        self.nbuf = 0

    def sb(self, shape, dt, name=None):
        self.nbuf += 1
        return Buf(self.nc.alloc_sbuf_tensor("sb_" + (name or ("%d" % self.nbuf)), list(shape), dt))

    def psb(self, shape, dt, name=None):
        self.nbuf += 1
        b = Buf(self.nc.alloc_psum_tensor("ps_" + (name or ("%d" % self.nbuf)), list(shape), dt))
        b.psum = True
        return b

    def _semh(self, key):
        if key[0] == "d":
            return self.dsems[key[1]]
        if key not in self.esem:
            self.esem[key] = self.nc.alloc_semaphore("s_%s_%d" % (key[1], key[2]))
        return self.esem[key]

    def _wait(self, eng, tok):
        if tok is None:
            return
        key, val = tok
        if key[0] == "e" and key[1] == "pe" and eng == "pe":
            return
        if self.seen[eng].get(key, 0) >= val:
            return
        self.seen[eng][key] = val
        sem = self._semh(key)
        self.prog[eng].append(lambda h, sem=sem, val=val: h.wait_ge(sem, val))

    def _deps(self, eng, reads, writes, self_raw_only=True):
        for b in reads:
            self._wait(eng, b.w)
        for b in writes:
            if b.w is not None and not (b.w[0][0] == "e" and b.w[0][1] == eng):
                self._wait(eng, b.w)
            for k, tok in b.r.items():
                if k[0] == "e" and k[1] == eng:
                    continue
                self._wait(eng, tok)

    def op(self, eng, fn, reads=(), writes=()):
        if self.dry:
            return
        writes = [b for b in writes if b is not None] + [b for b in reads if b is not None and b.psum]
        reads = [b for b in reads if b is not None and not b.psum]
        self._deps(eng, reads, writes)
        self.cnt[eng] += 1
        c = self.cnt[eng]
        key = ("e", eng, (c - 1) // EPOCH)
        val = (c - 1) % EPOCH + 1
        sem = self._semh(key)
        self.prog[eng].append(lambda h, fn=fn, sem=sem: fn(h).then_inc(sem, 1))
        tok = (key, val)
        for b in reads:
            b.r[key] = tok
        for b in writes:
            b.w = tok
            b.r = {}

    def dma(self, out, in_, queue="sp"):
        if self.dry:
            return
        reads = [in_.buf] if in_.buf.track else []
        writes = [out.buf] if out.buf.track else []
        self._deps(queue, reads, writes)
        k = self.dnext
        self.dnext = (k + 1) % NDS
        if self.dcnt[k] > 0:
            self._wait(queue, (("d", k), self.dcnt[k]))
        self.dcnt[k] += 16
        v = self.dcnt[k]
        sem = self.dsems[k]
        oap, iap = out.ap, in_.ap
        self.prog[queue].append(lambda h, oap=oap, iap=iap, sem=sem: h.dma_start(out=oap, in_=iap).then_inc(sem, 16))
        tok = (("d", k), v)
        for b in reads:
            b.r[("d", k)] = tok
        for b in writes:
            b.w = tok
            b.r = {}

    def finish(self):
        for k in range(NDS):
            if self.dcnt[k] > 0:
                self._wait("sp", (("d", k), self.dcnt[k]))

    def mm(self, groups):
        if self.dry:
            return
        reads, writes = [], []
        for out, pairs, st, sp in groups:
            writes.append(out.buf)
            for a, b in pairs:
                reads += [a.buf, b.buf]

        def fn(h, groups=groups):
            last = None
            for out, pairs, st, sp in groups:
                n = len(pairs)
                for i, (a, b) in enumerate(pairs):
                    last = h.matmul(out.ap, lhsT=a.ap, rhs=b.ap, start=(st and i == 0), stop=(sp and i == n - 1))
            return last
        self.op("pe", fn, reads, writes)

    def mm1(self, out, pairs, start=True, stop=True):
        self.mm([(out, pairs, start, stop)])

    def transposes(self, items):
        if self.dry:
            return
        reads, writes = [], []
        for o, i, idn in items:
            writes.append(o.buf)
            reads += [i.buf, idn.buf]

        def fn(h, items=items):
            last = None
            for o, i, idn in items:
                last = h.transpose(o.ap, i.ap, idn.ap)
            return last
        self.op("pe", fn, reads, writes)

    def act(self, out, in_, func, scale=None, bias=None, eng="act"):
        kw = {}
        reads = [in_.buf]
        if scale is not None:
            if isinstance(scale, V):
                kw["scale"] = scale.ap
                reads.append(scale.buf)
            else:
                kw["scale"] = float(scale)
        if bias is not None:
            if isinstance(bias, V):
                kw["bias"] = bias.ap
                reads.append(bias.buf)
            else:
                kw["bias"] = float(bias)
        self.op("act", lambda h: h.activation(out=out.ap, in_=in_.ap, func=func, **kw), reads, [out.buf])

    def tt(self, eng, out, in0, in1, op):
        self.op(eng, lambda h: h.tensor_tensor(out=out.ap, in0=in0.ap, in1=in1.ap, op=op), [in0.buf, in1.buf], [out.buf])

    def ts(self, eng, out, in0, s1, s2, op0, op1=None):
        reads = [in0.buf]
        a1 = s1
        a2 = s2
        if isinstance(s1, V):
            a1 = s1.ap
            reads.append(s1.buf)
        if isinstance(s2, V):
            a2 = s2.ap
            reads.append(s2.buf)
        if op1 is None:
            self.op(eng, lambda h: h.tensor_scalar(out=out.ap, in0=in0.ap, scalar1=a1, scalar2=None, op0=op0), reads, [out.buf])
        else:
            self.op(eng, lambda h: h.tensor_scalar(out=out.ap, in0=in0.ap, scalar1=a1, scalar2=a2, op0=op0, op1=op1), reads, [out.buf])

    def stt(self, out, in0, scalar, in1, op0, op1):
        reads = [in0.buf, in1.buf]
        sc = scalar
        if isinstance(scalar, V):
            sc = scalar.ap
            reads.append(scalar.buf)
        self.op("dve", lambda h: h.scalar_tensor_tensor(out=out.ap, in0=in0.ap, scalar=sc, in1=in1.ap, op0=op0, op1=op1), reads, [out.buf])

    def copy(self, eng, out, in_):
        if eng == "act":
            self.op("act", lambda h: h.copy(out=out.ap, in_=in_.ap), [in_.buf], [out.buf])
        else:
            self.op(eng, lambda h: h.tensor_copy(out=out.ap, in_=in_.ap), [in_.buf], [out.buf])

    def memset(self, eng, out, val):
        self.op(eng, lambda h: h.memset(out.ap, val), [], [out.buf])

    def scan(self, out, d0, d1, initial, op0, op1):
        self.op("dve", lambda h: h.tensor_tensor_scan(out=out.ap, data0=d0.ap, data1=d1.ap, initial=initial, op0=op0, op1=op1),
                [d0.buf, d1.buf], [out.buf])

    def reduce(self, out, in_, op, axis=None):
        axis = axis or AX.X
        self.op("dve", lambda h: h.tensor_reduce(out=out.ap, in_=in_.ap, axis=axis, op=op), [in_.buf], [out.buf])

class WStream:
    def __init__(self, K, nslots=13, nstage=2, look=9):
        self.K = K
        self.slots = [K.sb([128, 8, 128], BF16, "wslot%d" % i) for i in range(nslots)]
        self.f32p = None
        self.specs = []
        self.pos = 0
        self.issued = 0
        self.look = look
        self.scr = {}
        self.done = set()
        self.nstaged = 0
        self.pre = set()

    def _convert(self, parts, dst):
        K = self.K
        for (src, p, nk, c0, ncols, scale) in parts:
            k0 = 0
            while k0 < nk:
                nkh = min(4, nk - k0)
                pg = self.f32p.alloc()
                st = pg[0:p, 0:nkh * ncols].rearrange("p (k c) -> p k c", k=nkh)
                K.dma(st, V(src.buf, src.ap[:, k0:k0 + nkh, :]))
                o = dst[0:p, k0:k0 + nkh, c0:c0 + ncols]
                if scale is None:
                    K.copy("pool", o, st)
                else:
                    K.ts("pool", o, st, float(scale), None, ALU.mult)
                self.f32p.free(pg)
                k0 += nkh

    def _convert_pair(self, wide, nk, dst_a, dst_b, eng_b="pool"):
        K = self.K
        for k0 in range(0, nk, 2):
            pg = self.f32p.alloc()
            st = pg[:, 0:512].rearrange("p (k c) -> p k c", k=2)
            K.dma(st, V(wide.buf, wide.ap[:, k0:k0 + 2, :]))
            K.copy("pool", dst_a[:, k0:k0 + 2, :], st[:, :, 0:128])
            K.copy(eng_b, dst_b[:, k0:k0 + 2, :], st[:, :, 128:256])
            self.f32p.free(pg)

    def reset(self):
        self.pos = 0
        self.issued = 0
        seen = set()
        self.bg = []
        for key, parts, wide in self.specs:
            if key[1] >= 1 and key not in seen:
                seen.add(key)
                self.bg.append((key, parts, wide))
        self.bgpos = 0
        self.bgbuf = [self.K.sb([128, 8, 128], BF16, "wbg%d" % i) for i in range(4)]

    def background(self, n=1):
        K = self.K
        if K.dry:
            return
        while n > 0 and self.bgpos < len(self.bg):
            key, parts, wide = self.bg[self.bgpos]
            self.bgpos += 1
            if key in self.done:
                continue
            n -= 1
            p, nk = parts[0][1], parts[0][2]
            ext = max(c0 + ncols for (_, _, _, c0, ncols, _) in parts)
            buf = self.bgbuf[self.nstaged % len(self.bgbuf)]
            self.nstaged += 1
            if wide is not None and self.bgpos < len(self.bg) and self.bg[self.bgpos][0] not in self.done:
                key2 = self.bg[self.bgpos][0]
                self.bgpos += 1
                n -= 1
                buf2 = self.bgbuf[self.nstaged % len(self.bgbuf)]
                self.nstaged += 1
                self._convert_pair(wide, nk, buf, buf2)
                K.dma(self.scr[key][:, 0:nk, :], buf[:, 0:nk, :])
                K.dma(self.scr[key2][:, 0:nk, :], buf2[:, 0:nk, :])
                self.done.add(key)
                self.done.add(key2)
                continue
            self._convert(parts, buf)
            K.dma(self.scr[key][0:p, 0:nk, 0:ext], buf[0:p, 0:nk, 0:ext])
            self.done.add(key)

    def _issue(self, i):
        K = self.K
        if i in self.pre:
            return
        key, parts, wide = self.specs[i]
        slot = self.slots[i % len(self.slots)]
        p, nk = parts[0][1], parts[0][2]
        ext = max(c0 + ncols for (_, _, _, c0, ncols, _) in parts)
        scr = self.scr[key]
        if key in self.done:
            K.dma(slot[0:p, 0:nk, 0:ext], scr[0:p, 0:nk, 0:ext])
            return
        if wide is not None and i + 1 < len(self.specs) and self.specs[i + 1][0] not in self.done:
            key2 = self.specs[i + 1][0]
            slot2 = self.slots[(i + 1) % len(self.slots)]
            self._convert_pair(wide, nk, slot, slot2, eng_b="act")
            K.dma(scr[:, 0:nk, :], slot[:, 0:nk, :])
            K.dma(self.scr[key2][:, 0:nk, :], slot2[:, 0:nk, :])
            self.done.add(key)
            self.done.add(key2)
            self.pre.add(i + 1)
            return
        self._convert(parts, slot)
        K.dma(scr[0:p, 0:nk, 0:ext], slot[0:p, 0:nk, 0:ext])
        self.done.add(key)

    def get(self, parts, key, wide=None):
        if self.K.dry:
            self.specs.append((key, parts, wide))
            if key not in self.scr:
                self.scr[key] = Buf(self.K.nc.dram_tensor("wscr%d" % len(self.scr), [128, 8, 128], BF16).ap())
            return self.slots[0]
        i = self.pos
        self.pos += 1
        while self.issued < min(len(self.specs), i + 1 + self.look):
            self._issue(self.issued)
            self.issued += 1
        return self.slots[i % len(self.slots)]

def wview(dv, r0, nk, c0, ncols, p=128):
    return V(dv.buf, dv.ap[r0:r0 + nk * p, c0:c0 + ncols].rearrange("(k p) c -> p k c", p=p))

def build_program(debug=None):
    nc = bass.Bass("TRN2", target_bir_lowering=False)
    K = Ctx(nc)

    def din(name, shape):
        return Buf(nc.dram_tensor(name, list(shape), F32, kind="ExternalInput").ap(), track=False)

    def dout(name, shape):
        return Buf(nc.dram_tensor(name, list(shape), F32, kind="ExternalOutput").ap(), track=False)

    xp = din("xp", [SEQ, D])
    xsm = din("xsm", [SN, D])
    ckv = din("ckv", [DEPTH, NSEQ, PAST, 256])
    ckpe = din("ckpe", [DEPTH, NSEQ, PAST, 64])
    shg = din("shg", [DEPTH, NSEQ, 4, 64, 64])
    norm_g = din("norm_g", [DEPTH, 6, D])
    wgate = din("ffn_w_gate", [DEPTH, 2, D, DFF])
    wup = din("ffn_w_up", [DEPTH, 2, D, DFF])
    wdown = din("ffn_w_down", [DEPTH, 2, DFF, D])
    w_in = din("w_in", [DEPTH, D, 2240])
    w_out = din("w_out", [DEPTH, D, D])
    qa_g = din("mla_qa_g", [DEPTH, 384])
    wqb = din("mla_wqb", [DEPTH, 384, 768])
    kva_g = din("mla_kva_g", [DEPTH, 256])
    w_uk = din("mla_w_uk", [DEPTH, 256, 4, 128])
    w_uv = din("mla_w_uv", [DEPTH, 256, 4, 128])
    lb_logits = din("hgrn_lb_logits", [DEPTH, 256])
    hog = din("hgrn_out_g", [DEPTH, 256])
    cvg = din("cmlp_v_g", [DEPTH, 256])
    cws = din("cmlp_w_s", [DEPTH, 4, 128, 128])
    cbs = din("cmlp_b_s", [DEPTH, 4, 128])
    cstd = din("cst", [128, 1024])
    roped = din("rope", [64, 2, SEQ + SN])

    yp = dout("yp", [SEQ, D])
    ysm = dout("ysm", [SN, D])
    ckv_p = dout("ckv_p", [DEPTH, SEQ, 256])
    kpe_p = dout("kpe_p", [DEPTH, SEQ, 64])
    hg_p = dout("hg_p", [DEPTH, 4, 64, 64])
    ckv_s = dout("ckv_s", [DEPTH, SN, 256])
    kpe_s = dout("kpe_s", [DEPTH, SN, 64])
    hg_s = dout("hg_s", [DEPTH, NSEQ, 4, 64, 64])
    cv_s = dout("cv_s", [DEPTH, SN, 256])
    xscr = [Buf(nc.dram_tensor("xscr%d" % t, [128, 8, TN], F32).ap()) for t in range(NPT + 1)]
    dbg = {}
    if debug:
        for name, shape in debug.items():
            dbg[name] = dout("dbg_" + name, shape)

    xs = [K.sb([128, TN], F32, "xs%d" % k) for k in range(8)]
    hb = [K.sb([128, TN], BF16, "hb%d" % k) for k in range(8)]
    kT = K.sb([128, 3, SEQ], BF16, "kT")
    ctok = K.sb([128, SEQ // 128, 256], BF16, "ctok")
    hit = K.sb([64, 8, 256], BF16, "hit")
    kdt = K.sb([64, 8, 256], BF16, "kdt")
    cst = K.sb([128, 1024], F32, "cst")
    colv = K.sb([128, 128], F32, "colv")
    colvh = K.sb([128, 128], F32, "colvh")
    onesb = K.sb([128, 128], BF16, "onesb")
    identb = K.sb([128, 128], BF16, "identb")
    zerob = K.sb([128, 128], BF16, "zerob")
    epsc = K.sb([128, 1], F32, "epsc")
    wukT = K.sb([128, 4, 256], BF16, "wukT")
    wuv = K.sb([128, 2, 512], BF16, "wuv")
    wTm = K.sb([128, 4, 128], BF16, "wTm")
    biasb = K.sb([64, 4, 128], F32, "biasb")
    vgn = K.sb([128, 256], F32, "vgn")
    lbc = K.sb([64, 16], F32, "lbc")
    S32 = K.sb([64, 4, 64], F32, "S32")
    Sbf = K.sb([64, 4, 64], BF16, "Sbf")
    kmx = K.sb([128, 2], F32, "kmx")
    NF32, NB16 = 20, 46
    f32p = Pool([K.sb([128, TN], F32, "f%d" % i) for i in range(NF32)], "f32")
    b16p = Pool([K.sb([128, TN], BF16, "b%d" % i) for i in range(NB16)], "b16")
    psp = Pool([K.psb([128, TN], F32, "psum%d" % i) for i in range(8)], "psum")
    ws = WStream(K)
    ws.f32p = f32p

    ident = cst[:, 0:128]
    tril = cst[:, 128:256]
    triu4 = cst[0:64, 256:512]
    rmask = cst[:, 512:1024]

    def gcol(l, i, k):
        r = R_NG + (l * 6 + i) * 8 + k
        return colv[:, r:r + 1]

    def gcolh(l, i, k):
        r = R_NG + (l * 6 + i) * 8 + k
        return colvh[:, r:r + 1]

    def norm_stats(src, nk, n, inv_count, P=128):
        ps = psp.alloc()
        for k in range(nk):
            sq = b16p.alloc()
            K.act(sq[0:P, :n], src(k), AF.Square)
            K.mm1(ps[0:P, :n], [(onesb[0:P, 0:P], sq[0:P, :n])], start=(k == 0), stop=(k == nk - 1))
            b16p.free(sq)
        r = f32p.alloc()
        K.act(r[0:P, :n], ps[0:P, :n], AF.Ln, scale=inv_count, bias=epsc[0:P, 0:1])
        psp.free(ps)
        K.act(r[0:P, :n], r[0:P, :n], AF.Exp, scale=-0.5)
        return r

    def prenorm(l, i, n):
        r = norm_stats(lambda k: xs[k][:, :n], 8, n, 1.0 / D)
        for k in range(8):
            K.stt(hb[k][:, :n], xs[k][:, :n], gcol(l, i, k), r[:, :n], ALU.mult, ALU.mult)
        f32p.free(r)

    def out_proj_epilogue(ys, ssq, l, i, n, half):
        flush_ssq()
        r = f32p.alloc()
        K.act(r[:, :n], ssq[:, :n], AF.Ln, scale=1.0 / D, bias=epsc[:, 0:1])
        psp.free(ssq)
        K.act(r[:, :n], r[:, :n], AF.Exp, scale=-0.5)
        for m in range(8):
            t = f32p.alloc()
            K.stt(t[:, :n], ys[m][:, :n], gcol(l, i, m), r[:, :n], ALU.mult, ALU.mult)
            K.stt(xs[m][:, :n], t[:, :n], float(half), xs[m][:, :n], ALU.mult, ALU.add)
            f32p.free(t)
            f32p.free(ys[m])
        f32p.free(r)

    pend_ssq = []

    def flush_ssq():
        while pend_ssq:
            pend_ssq.pop(0)()

    def evac_y(py, ys, ssq, m, n):
        flush_ssq()
        y = f32p.alloc()
        ys.append(y)
        sq = b16p.alloc()
        K.act(sq[:, :n], py[:, :n], AF.Square)
        K.copy("dve", y[:, :n], py[:, :n])
        psp.free(py)

        def later():
            K.mm1(ssq[:, :n], [(onesb[:, :], sq[:, :n])], start=(m == 0), stop=(m == 7))
            b16p.free(sq)
        pend_ssq.append(later)

    def ffn(l, fi, n):
        ni_pre, ni_post = (0, 1) if fi == 0 else (4, 5)
        prenorm(l, ni_pre, n)
        gs = []
        for j0 in range(0, NJ, 2):
            wg = []
            for jj in (j0, j0 + 1):
                wg.append(ws.get([(wview(wgate[l, fi], 0, 8, jj * 128, 128), 128, 8, 0, 128, None)], ("g", l, fi, jj),
                                 wide=(wview(wgate[l, fi], 0, 8, j0 * 128, 256) if jj == j0 else None)))
            pgs = []
            for q in range(2):
                pg = psp.alloc()
                if j0 == 0 and q == 0:
                    for k in range(8):
                        K.mm1(pg[:, :n], [(wg[q][:, k, :], hb[k][:, :n])], start=(k == 0), stop=(k == 7))
                else:
                    K.mm1(pg[:, :n], [(wg[q][:, k, :], hb[k][:, :n]) for k in range(8)])
                pgs.append(pg)
            wu = []
            for jj in (j0, j0 + 1):
                wu.append(ws.get([(wview(wup[l, fi], 0, 8, jj * 128, 128), 128, 8, 0, 128, None)], ("u", l, fi, jj),
                                 wide=(wview(wup[l, fi], 0, 8, j0 * 128, 256) if jj == j0 else None)))
            for q in range(2):
                pg = pgs[q]
                pu = psp.alloc()
                K.mm1(pu[:, :n], [(wu[q][:, k, :], hb[k][:, :n]) for k in range(8)])
                sg = f32p.alloc()
                K.act(sg[:, :n], pg[:, :n], AF.Silu)
                psp.free(pg)
                gj = b16p.alloc()
                K.tt("dve", gj[:, :n], sg[:, :n], pu[:, :n], ALU.mult)
                psp.free(pu)
                f32p.free(sg)
                gs.append(gj)
                if l + 1 < cfg["layers"]:
                    ws.background(1)
        ssq = psp.alloc()
        ys = []
        for m0 in range(0, 8, 2):
            py = [psp.alloc(), psp.alloc()]
            for pi, (r0, nk) in enumerate(((0, 8), (1024, 8), (2048, 6))):
                w = []
                for q in range(2):
                    m = m0 + q
                    w.append(ws.get([(wview(wdown[l, fi], r0, nk, m * 128, 128), 128, nk, 0, 128, None)], ("d", l, fi, m, r0),
                                    wide=(wview(wdown[l, fi], r0, nk, m0 * 128, 256) if q == 0 else None)))
                for q in range(2):
                    K.mm1(py[q][:, :n], [(w[q][:, k, :], gs[pi * 8 + k][:, :n]) for k in range(nk)], start=(pi == 0), stop=(pi == 2))
            for q in range(2):
                evac_y(py[q], ys, ssq, m0 + q, n)
        b16p.free(*gs)
        out_proj_epilogue(ys, ssq, l, ni_post, n, 0.5)

    def win_slot(l, c0, ncols):
        return ws.get([(wview(w_in[l], 0, 8, c0, ncols), 128, 8, 0, ncols, None)], ("in", l, c0))

    def win_rot_slot(l, c0):
        return ws.get([(wview(w_in[l], 0, 8, c0 + 32, 32), 128, 8, 0, 32, -1.0),
                       (wview(w_in[l], 0, 8, c0, 32), 128, 8, 32, 32, None)], ("inrot", l, c0))

    def proj_fm(wslot, nk, M, n, src=None, srcP=128):
        ps = psp.alloc()
        if src is None:
            src = lambda k: hb[k][:, :n]
        K.mm1(ps[0:M, :n], [(wslot[0:srcP, k, 0:M], src(k)) for k in range(nk)])
        return ps

    def rope_from_psum(pp, ppr, cs, n, out):
        t1 = f32p.alloc()
        t2 = f32p.alloc()
        K.tt("dve", t1[0:64, :n], pp[0:64, :n], cs[0][0:64, :n], ALU.mult)
        psp.free(pp)
        K.tt("dve", t2[0:64, :n], ppr[0:64, :n], cs[1][0:64, :n], ALU.mult)
        psp.free(ppr)
        K.tt("dve", out, t1[0:64, :n], t2[0:64, :n], ALU.add)
        f32p.free(t1, t2)

    def attend_gen(h, qT, n_q, blocks, mixa_dst, pend):
        po = [psp.alloc(), psp.alloc()]
        pd = psp.alloc()
        nb = len(blocks)

        def qk(bi):
            kf, cf, nk, c0, diag = blocks[bi]
            pst = psp.alloc()
            K.mm1(pst[0:nk, c0:n_q], [(kf(0), qT[3 * h][:, c0:n_q]), (kf(1), qT[3 * h + 1][:, c0:n_q]),
                                      (kf(2), qT[3 * h + 2][0:65, c0:n_q])])
            pT = b16p.alloc()
            K.act(pT[0:nk, c0:n_q], pst[0:nk, c0:n_q], AF.Exp, scale=MLA_SCALE)
            psp.free(pst)
            if diag:
                K.memset("pool", pT[64:128, c0:c0 + 64], 0.0)
            return pT

        nxt = qk(0)
        for bi, (kf, cf, nk, c0, diag) in enumerate(blocks):
            pT = nxt
            if bi + 1 < nb:
                nxt = qk(bi + 1)
            first, last = bi == 0, bi == nb - 1
            K.mm([(po[0][:, c0:n_q], [(cf(0), pT[0:nk, c0:n_q])], first, last),
                  (po[1][:, c0:n_q], [(cf(1), pT[0:nk, c0:n_q])], first, last),
                  (pd[:, c0:n_q], [(onesb[0:nk, :], pT[0:nk, c0:n_q])], first, last)])
            b16p.free(pT)
            if bi == 1 and pend:
                pend.pop()()
            yield
        if pend:
            pend.pop()()

        def fin():
            rd = f32p.alloc()
            K.act(rd[:, :n_q], pd[:, :n_q], AF.Ln)
            psp.free(pd)
            K.act(rd[:, :n_q], rd[:, :n_q], AF.Exp, scale=-1.0)
            ol = [b16p.alloc(), b16p.alloc()]
            for cc in range(2):
                K.tt("dve", ol[cc][:, :n_q], po[cc][:, :n_q], rd[:, :n_q], ALU.mult)
                psp.free(po[cc])
            f32p.free(rd)
            pv = psp.alloc()
            K.mm1(pv[:, :n_q], [(wuv[:, cc, h * 128:(h + 1) * 128], ol[cc][:, :n_q]) for cc in range(2)])
            b16p.free(*ol)
            K.copy("act", mixa_dst, pv[:, :n_q])
            psp.free(pv)
        pend.append(fin)

    def key_norm_update(kf, nkeys, first):
        ps = psp.alloc()
        for f in range(3):
            P = 128 if f < 2 else 64
            sq = b16p.alloc()
            K.act(sq[0:P, :nkeys], kf(f)[0:P], AF.Square)
            K.mm1(ps[:, :nkeys], [(onesb[0:P, :], sq[0:P, :nkeys])], start=(f == 0), stop=(f == 2))
            b16p.free(sq)
        t = f32p.alloc()
        K.act(t[:, :nkeys], ps[:, :nkeys], AF.Sqrt)
        psp.free(ps)
        m = f32p.alloc()
        K.reduce(m[:, 0:1], t[:, :nkeys], ALU.max)
        if first:
            K.copy("dve", kmx[:, 0:1], m[:, 0:1])
        else:
            K.tt("dve", kmx[:, 0:1], kmx[:, 0:1], m[:, 0:1], ALU.max)
        K.ts("dve", kmx[:, 1:2], kmx[:, 0:1], -1.0, None, ALU.mult)
        f32p.free(t, m)

    def mla(l, ti, n, sample):
        tok0 = ti * TN
        cs = [f32p.alloc(), f32p.alloc()]
        rc0 = SEQ if sample else tok0
        K.dma(cs[0][0:64, :n], roped[:, 0, rc0:rc0 + n])
        K.dma(cs[1][0:64, :n], roped[:, 1, rc0:rc0 + n])
        pkv = [proj_fm(win_slot(l, O_KV + 128 * c, 128), 8, 128, n) for c in range(2)]
        rkv = norm_stats(lambda k: pkv[k][:, :n], 2, n, 1.0 / 256)
        cn = [f32p.alloc(), f32p.alloc()]
        for c in range(2):
            K.stt(cn[c][:, :n], pkv[c][:, :n], colv[:, R_KVG + l * 2 + c:R_KVG + l * 2 + c + 1], rkv[:, :n], ALU.mult, ALU.mult)
            psp.free(pkv[c])
        f32p.free(rkv)
        yield
        pk = proj_fm(win_slot(l, O_KPE, 64), 8, 64, n)
        pkr = proj_fm(win_rot_slot(l, O_KPE), 8, 64, n)
        kpe = f32p.alloc()
        rope_from_psum(pk, pkr, cs, n, kpe[0:64, :n])
        yield
        if sample:
            kTn = [b16p.alloc() for _ in range(3)]
            kdst = lambda f: kTn[f][:, :n]
            K.memset("pool", kTn[2][64:65, :n], 1.0)
        else:
            kTn = None
            kdst = lambda f: kT[:, f, tok0:tok0 + n]
        K.copy("pool", kdst(0), cn[0][:, :n])
        K.copy("pool", kdst(1), cn[1][:, :n])
        K.copy("pool", kdst(2)[0:64], kpe[0:64, :n])
        if not sample:
            for tb in range(n // 128):
                pt = psp.alloc()
                sl = slice(tb * 128, (tb + 1) * 128)
                K.transposes([(pt[:, 0:128], cn[0][:, sl], ident), (pt[:, 128:256], cn[1][:, sl], ident),
                              (pt[:, 256:320], kpe[0:64, sl], ident[0:64, 0:64])])
                ct = f32p.alloc()
                K.copy("act", ct[:, 0:320], pt[:, 0:320])
                K.copy("dve", ctok[:, ti * 4 + tb, :], pt[:, 0:256])
                psp.free(pt)
                K.dma(ckv_p[l, tok0 + tb * 128:tok0 + (tb + 1) * 128, :], ct[:, 0:256])
                K.dma(kpe_p[l, tok0 + tb * 128:tok0 + (tb + 1) * 128, :], ct[:, 256:320])
                f32p.free(ct)
                yield
            ctn = None
        else:
            ctn = [b16p.alloc() for _ in range(NSEQ // 2)]
            for s in range(NSEQ):
                pt = psp.alloc()
                sl = slice(s * 64, (s + 1) * 64)
                K.transposes([(pt[0:64, 0:128], cn[0][:, sl], ident), (pt[0:64, 128:256], cn[1][:, sl], ident),
                              (pt[0:64, 256:320], kpe[0:64, sl], ident[0:64, 0:64])])
                ct = f32p.alloc()
                K.copy("act", ct[0:64, 0:320], pt[0:64, 0:320])
                K.copy("dve", ctn[s // 2][0:64, (s % 2) * 256:(s % 2) * 256 + 256], pt[0:64, 0:256])
                psp.free(pt)
                K.dma(ckv_s[l, s * 64:(s + 1) * 64, :], ct[0:64, 0:256])
                K.dma(kpe_s[l, s * 64:(s + 1) * 64, :], ct[0:64, 256:320])
                f32p.free(ct)
        f32p.free(cn[0], cn[1], kpe)
        if not sample:
            key_norm_update(lambda f: kT[:, f, tok0:tok0 + n], n, ti == 0)
        yield
        pq = [proj_fm(win_slot(l, O_QA + 128 * c, 128), 8, 128, n) for c in range(3)]
        rq = norm_stats(lambda k: pq[k][:, :n], 3, n, 1.0 / 384)
        qan = [b16p.alloc() for _ in range(3)]
        for c in range(3):
            K.stt(qan[c][:, :n], pq[c][:, :n], colv[:, R_QAG + l * 3 + c:R_QAG + l * 3 + c + 1], rq[:, :n], ALU.mult, ALU.mult)
            psp.free(pq[c])
        f32p.free(rq)
        qT = [b16p.alloc() for _ in range(12)]
        qsrc = lambda k: qan[k][:, :n]
        for h in range(4):
            wn = ws.get([(wview(wqb[l], 0, 3, 192 * h, 128), 128, 3, 0, 128, None)], ("qn", l, h))
            wp = ws.get([(wview(wqb[l], 0, 3, 192 * h + 128, 64), 128, 3, 0, 64, None)], ("qp", l, h))
            wpr = ws.get([(wview(wqb[l], 0, 3, 192 * h + 160, 32), 128, 3, 0, 32, -1.0),
                          (wview(wqb[l], 0, 3, 192 * h + 128, 32), 128, 3, 32, 32, None)], ("qpr", l, h))
            pn = proj_fm(wn, 3, 128, n, qsrc)
            qn = b16p.alloc()
            K.copy("act", qn[:, :n], pn[:, :n])
            psp.free(pn)
            pp = proj_fm(wp, 3, 64, n, qsrc)
            ppr = proj_fm(wpr, 3, 64, n, qsrc)
            rope_from_psum(pp, ppr, cs, n, qT[3 * h + 2][0:64, :n])
            for cc in range(2):
                pl = psp.alloc()
                K.mm1(pl[:, :n], [(wukT[:, h, cc * 128:(cc + 1) * 128], qn[:, :n])])
                K.copy("act", qT[3 * h + cc][:, :n], pl[:, :n])
                psp.free(pl)
            b16p.free(qn)
            yield
        b16p.free(*qan)
        f32p.free(*cs)

        def q_bound(h, c0, c1):
            ps = psp.alloc()
            w = c1 - c0
            for f in range(3):
                P = 128 if f < 2 else 64
                sq = b16p.alloc()
                K.act(sq[0:P, :w], qT[3 * h + f][0:P, c0:c1], AF.Square)
                K.mm1(ps[:, :w], [(onesb[0:P, :], sq[0:P, :w])], start=(f == 0), stop=(f == 2))
                b16p.free(sq)
            t = f32p.alloc()
            K.act(t[:, :w], ps[:, :w], AF.Sqrt)
            psp.free(ps)
            K.ts("dve", qT[3 * h + 2][64:65, c0:c1], t[64:65, :w], kmx[64:65, 1:2], None, ALU.mult)
            f32p.free(t)

        mixa = [b16p.alloc() for _ in range(4)]
        if not sample:
            for h in range(4):
                q_bound(h, 0, n)

            def attn_gen():
                pend = []
                for h in range(4):
                    blocks = []
                    for kb in range(4 * ti + 4):
                        i = kb - 4 * ti
                        c0 = 128 * i if i > 0 else 0
                        ksl = slice(kb * 128, (kb + 1) * 128)
                        blocks.append((lambda f, ksl=ksl: kT[0:(128 if f < 2 else 65), f, ksl],
                                       lambda cc, kb=kb: ctok[:, kb, cc * 128:(cc + 1) * 128], 128, c0, i >= 0))
                    yield from attend_gen(h, qT, n, blocks, mixa[h][:, :n], pend)
                pend.pop()()
                b16p.free(*qT)
            return mixa, attn_gen(), 4 * (4 * ti + 4)
        else:
            for s in range(NSEQ):
                qsl = slice(s * 64, (s + 1) * 64)

                def load_block(kb):
                    kc = f32p.alloc()
                    K.dma(kc[:, 0:256], ckv[l, s, kb * 128:(kb + 1) * 128, :])
                    K.dma(kc[:, 256:320], ckpe[l, s, kb * 128:(kb + 1) * 128, :])
                    pt = psp.alloc()
                    K.transposes([(pt[:, 0:128], kc[:, 0:128], ident), (pt[:, 128:256], kc[:, 128:256], ident),
                                  (pt[0:64, 256:384], kc[:, 256:320], ident)])
                    kts = b16p.alloc()
                    K.copy("act", kts[:, 0:256], pt[:, 0:256])
                    K.copy("act", kts[0:64, 256:384], pt[0:64, 256:384])
                    psp.free(pt)
                    return kc, kts
                key_norm_update(lambda f: kTn[f][0:(128 if f < 2 else 64), qsl], 64, True)
                for kb in range(PAST // 128):
                    kc, kts = load_block(kb)
                    key_norm_update(lambda f, kts=kts: kts[0:(128 if f < 2 else 64), f * 128:(f + 1) * 128], 128, False)
                    f32p.free(kc)
                    b16p.free(kts)
                for h in range(4):
                    q_bound(h, s * 64, (s + 1) * 64)
                po = [psp.alloc(), psp.alloc()]
                pd = psp.alloc()
                K.mm([(po[0][:, 0:256], [(zerob[:, :], qT[0][:, 0:256])], True, False),
                      (po[1][:, 0:256], [(zerob[:, :], qT[0][:, 0:256])], True, False),
                      (pd[:, 0:256], [(zerob[:, :], qT[0][:, 0:256])], True, False)])
                nblk = PAST // 128 + 1
                for kb in range(nblk):
                    if kb < nblk - 1:
                        kc, kts = load_block(kb)
                        K.memset("pool", kts[64:65, 256:384], 1.0)
                        ctk = b16p.alloc()
                        K.copy("pool", ctk[:, 0:256], kc[:, 0:256])
                        f32p.free(kc)
                        kf = lambda f, kts=kts: kts[0:(128 if f < 2 else 65), f * 128:(f + 1) * 128]
                        cf = lambda cc, ctk=ctk: ctk[:, cc * 128:(cc + 1) * 128]
                        nk = 128
                    else:
                        kts = ctk = None
                        kf = lambda f: kTn[f][0:(128 if f < 2 else 65), qsl]
                        cf = lambda cc: ctn[s // 2][0:64, (s % 2) * 256 + cc * 128:(s % 2) * 256 + (cc + 1) * 128]
                        nk = 64
                    pst = psp.alloc()
                    K.mm([(pst[0:nk, h * 64:(h + 1) * 64], [(kf(0), qT[3 * h][:, qsl]), (kf(1), qT[3 * h + 1][:, qsl]),
                                                             (kf(2), qT[3 * h + 2][0:65, qsl])], True, True) for h in range(4)])
                    pT = b16p.alloc()
                    K.act(pT[0:nk, 0:256], pst[0:nk, 0:256], AF.Exp, scale=MLA_SCALE)
                    psp.free(pst)
                    last = kb == nblk - 1
                    K.mm([(po[0][:, 0:256], [(cf(0), pT[0:nk, 0:256])], False, last),
                          (po[1][:, 0:256], [(cf(1), pT[0:nk, 0:256])], False, last),
                          (pd[:, 0:256], [(onesb[0:nk, :], pT[0:nk, 0:256])], False, last)])
                    b16p.free(pT)
                    if kts is not None:
                        b16p.free(kts, ctk)
                rd = f32p.alloc()
                K.act(rd[:, 0:256], pd[:, 0:256], AF.Ln)
                psp.free(pd)
                K.act(rd[:, 0:256], rd[:, 0:256], AF.Exp, scale=-1.0)
                ol = [b16p.alloc(), b16p.alloc()]
                for cc in range(2):
                    K.tt("dve", ol[cc][:, 0:256], po[cc][:, 0:256], rd[:, 0:256], ALU.mult)
                    psp.free(po[cc])
                f32p.free(rd)
                pv = psp.alloc()
                K.mm([(pv[:, h * 64:(h + 1) * 64], [(wuv[:, cc, h * 128:(h + 1) * 128], ol[cc][:, h * 64:(h + 1) * 64]) for cc in range(2)],
                       True, True) for h in range(4)])
                b16p.free(*ol)
                for h in range(4):
                    K.copy("act", mixa[h][:, qsl], pv[:, h * 64:(h + 1) * 64])
                psp.free(pv)
            b16p.free(*kTn)
            b16p.free(*ctn)
        b16p.free(*qT)
        return mixa, None, 0

    def hgrn_prep(l, ti, n, sample):
        nch = n // 64
        hq, fbs = [], []
        for h in range(4):
            p = proj_fm(win_slot(l, O_HQ + 64 * h, 64), 8, 64, n)
            t = f32p.alloc()
            K.copy("act", t[0:64, :n], p[0:64, :n])
            psp.free(p)
            hq.append(t)
        for h in range(4):
            p = proj_fm(win_slot(l, O_HF + 64 * h, 64), 8, 64, n)
            t = f32p.alloc()
            K.act(t[0:64, :n], p[0:64, :n], AF.Sigmoid)
            psp.free(p)
            fbs.append(t)
        w0 = win_slot(l, O_HI, 128)
        w1 = win_slot(l, O_HI + 128, 128)
        for j in range(nch):
            ps = psp.alloc()
            K.mm([(ps[0:64, 0:128], [(hb[k][:, 64 * j:64 * j + 64], w0[:, k, :]) for k in range(8)], True, True),
                  (ps[0:64, 128:256], [(hb[k][:, 64 * j:64 * j + 64], w1[:, k, :]) for k in range(8)], True, True)])
            K.copy("dve" if j % 2 else "act", hit[:, j, :], ps[0:64, 0:256])
            psp.free(ps)
        hgs = []
        for h in range(4):
            p = proj_fm(win_slot(l, O_HG + 64 * h, 64), 8, 64, n)
            t = b16p.alloc()
            K.act(t[0:64, :n], p[0:64, :n], AF.Silu)
            psp.free(p)
            hgs.append(t)
        st = dict(l=l, ti=ti, n=n, sample=sample, nch=nch, hgs=hgs, hq=hq, fbs=fbs)
        return st

    def hgrn_elem_gen(st):
        l, n, nch, hq, fbs = st["l"], st["n"], st["nch"], st["hq"], st["fbs"]
        kk, bb = [], []
        for h in range(4):
            t = fbs[h]
            K.ts("dve", t[0:64, :n], t[0:64, :n], lbc[:, 8 + l * 4 + h:8 + l * 4 + h + 1], lbc[:, l * 4 + h:l * 4 + h + 1], ALU.mult, ALU.add)
            k_ = f32p.alloc()
            K.ts("dve", k_[0:64, :n], t[0:64, :n], -1.0, 1.0, ALU.mult, ALU.add)
            K.act(t[0:64, :n], t[0:64, :n], AF.Ln)
            b_ = f32p.alloc()
            K.scan(b_[0:64, :n], rmask[0:64, :n], t[0:64, :n], 0.0, ALU.mult, ALU.add)
            f32p.free(t)
            kk.append(k_)
            bb.append(b_)
            yield
        eb, qe, ke, kd = [], [], [], []
        for h in range(4):
            e = f32p.alloc()
            K.act(e[0:64, :n], bb[h][0:64, :n], AF.Exp)
            q = b16p.alloc()
            K.tt("dve", q[0:64, :n], hq[h][0:64, :n], e[0:64, :n], ALU.mult)
            K.act(hq[h][0:64, :n], bb[h][0:64, :n], AF.Exp, scale=-1.0)
            K.tt("dve", kk[h][0:64, :n], kk[h][0:64, :n], hq[h][0:64, :n], ALU.mult)
            k2 = b16p.alloc()
            K.copy("pool", k2[0:64, :n], kk[h][0:64, :n])
            d2 = b16p.alloc()
            for j in range(nch):
                K.ts("dve", d2[0:64, 64 * j:64 * j + 64], kk[h][0:64, 64 * j:64 * j + 64], e[0:64, 64 * j + 63:64 * j + 64], None, ALU.mult)
            f32p.free(hq[h], bb[h], kk[h])
            eb.append(e)
            qe.append(q)
            ke.append(k2)
            kd.append(d2)
            yield
        st.update(eb=eb, qe=qe, ke=ke, kd=kd)

    def hgrn_kdT(st):
        nch, kd = st["nch"], st["kd"]
        for j in range(nch):
            ps = psp.alloc()
            pb = ps[0:64, 0:128].bitcast(BF16)
            K.transposes([(pb[:, h * 64:(h + 1) * 64], kd[h][0:64, 64 * j:64 * j + 64], identb[0:64, 0:64]) for h in range(4)])
            K.copy("act", kdt[:, j, :], pb[:, 0:256])
            psp.free(ps)
        b16p.free(*kd)

    def hgrn_chain_gen(st):
        l, ti, n, sample, nch, eb, qe, ke = st["l"], st["ti"], st["n"], st["sample"], st["nch"], st["eb"], st["qe"], st["ke"]
        ohg = [f32p.alloc() for _ in range(4)]
        st["ohg"] = ohg
        if (not sample) and ti == 0:
            K.memset("dve", S32[:, :, :], 0.0)
            K.memset("pool", Sbf[:, :, :], 0.0)
        for j in range(nch):
            jc = slice(64 * j, 64 * j + 64)
            if sample:
                K.dma(S32[:, :, :], shg[l, j].rearrange("h e d -> e h d"))
                K.copy("dve", Sbf[:, :, :], S32[:, :, :])
            pa = psp.alloc()
            K.mm([(pa[0:64, h * 64:(h + 1) * 64], [(ke[h][0:64, jc], qe[h][0:64, jc])], True, True) for h in range(4)])
            atm = b16p.alloc()
            K.tt("dve", atm[0:64, 0:256], pa[0:64, 0:256], triu4, ALU.mult)
            psp.free(pa)
            po = psp.alloc()
            K.mm([(po[0:64, h * 64:(h + 1) * 64], [(Sbf[:, h, :], qe[h][0:64, jc]),
                                                    (hit[:, j, h * 64:(h + 1) * 64], atm[0:64, h * 64:(h + 1) * 64])], True, True)
                  for h in range(4)])
            b16p.free(atm)
            for h in range(4):
                K.copy("act", ohg[h][0:64, jc], po[0:64, h * 64:(h + 1) * 64])
            psp.free(po)
            pn = psp.alloc()
            K.mm([(pn[0:64, h * 64:(h + 1) * 64], [(kdt[:, j, h * 64:(h + 1) * 64], hit[:, j, h * 64:(h + 1) * 64])], True, True)
                  for h in range(4)])
            for h in range(4):
                K.stt(S32[:, h, :], S32[:, h, :], eb[h][0:64, 64 * j + 63:64 * j + 64], pn[0:64, h * 64:(h + 1) * 64], ALU.mult, ALU.add)
            psp.free(pn)
            K.copy("dve", Sbf[:, :, :], S32[:, :, :])
            if sample:
                K.dma(hg_s[l, j].rearrange("h e d -> e h d"), S32[:, :, :])
            yield
        if (not sample) and ti == NPT - 1:
            K.dma(hg_p[l].rearrange("h e d -> e h d"), S32[:, :, :])
        b16p.free(*qe)
        b16p.free(*ke)
        f32p.free(*eb)

    def hgrn_fin(st):
        l, n, ohg, hgs = st["l"], st["n"], st["ohg"], st["hgs"]
        mixb = []
        for h in range(4):
            r = norm_stats(lambda k: ohg[h][0:64, :n], 1, n, 1.0 / 64, P=64)
            K.stt(ohg[h][0:64, :n], ohg[h][0:64, :n], colv[0:64, R_HOG + l * 4 + h:R_HOG + l * 4 + h + 1], r[0:64, :n], ALU.mult, ALU.mult)
            f32p.free(r)
            m = b16p.alloc()
            K.tt("dve", m[0:64, :n], ohg[h][0:64, :n], hgs[h][0:64, :n], ALU.mult)
            mixb.append(m)
        f32p.free(*ohg)
        b16p.free(*hgs)
        return mixb

    def cmlp_a(l, ti, n, sample):
        bs = 64 if sample else 128
        nb = n // bs
        us = []
        for g in range(4):
            p = proj_fm(win_slot(l, O_CU + 64 * g, 64), 8, 64, n)
            t = b16p.alloc()
            K.act(t[0:64, :n], p[0:64, :n], AF.Gelu_apprx_tanh)
            psp.free(p)
            us.append(t)
        w0 = win_slot(l, O_CV, 128)
        w1 = win_slot(l, O_CV + 128, 128)
        vgs = []
        for tb in range(nb):
            tsl = slice(tb * bs, (tb + 1) * bs)
            ps = psp.alloc()
            K.mm([(ps[0:bs, 0:128], [(hb[k][:, tsl], w0[:, k, :]) for k in range(8)], True, True),
                  (ps[0:bs, 128:256], [(hb[k][:, tsl], w1[:, k, :]) for k in range(8)], True, True)])
            vg = f32p.alloc()
            K.act(vg[0:bs, 0:256], ps[0:bs, 0:256], AF.Gelu_apprx_tanh)
            psp.free(ps)
            vgs.append(vg)
        return dict(l=l, n=n, sample=sample, bs=bs, nb=nb, us=us, vgs=vgs)

    def cmlp_b(cs_):
        l, n, sample, bs, nb, us, vgs = cs_["l"], cs_["n"], cs_["sample"], cs_["bs"], cs_["nb"], cs_["us"], cs_["vgs"]
        vts = []
        for tb in range(nb):
            vg = vgs[tb]
            sq = f32p.alloc()
            K.tt("pool", sq[0:bs, 0:256], vg[0:bs, 0:256], vg[0:bs, 0:256], ALU.mult)
            ss = f32p.alloc()
            K.reduce(ss[0:bs, 0:4], sq[0:bs, 0:256].rearrange("p (g d) -> p g d", g=4), ALU.add)
            K.act(ss[0:bs, 0:4], ss[0:bs, 0:4], AF.Ln, scale=1.0 / 64, bias=epsc[0:bs, 0:1])
            K.act(ss[0:bs, 0:4], ss[0:bs, 0:4], AF.Exp, scale=-0.5)
            vv = vg[0:bs, 0:256].rearrange("p (g d) -> p g d", g=4)
            K.tt("dve", vv, vv, ss[0:bs, 0:4].rearrange("p (g o) -> p g o", o=1).bcast([bs, 4, 64]), ALU.mult)
            K.tt("dve", vg[0:bs, 0:256], vg[0:bs, 0:256], vgn[0:bs, :], ALU.mult)
            if sample:
                K.dma(cv_s[l, tb * 64:(tb + 1) * 64, :], vg[0:64, 0:256])
            vt = b16p.alloc()
            K.copy("pool", vt[0:bs, 0:256], vg[0:bs, 0:256])
            f32p.free(vg, sq, ss)
            vts.append(vt)
        cs_["vts"] = vts

    def cmlp_c(cs_):
        l, n, sample, bs, nb, us, vts = cs_["l"], cs_["n"], cs_["sample"], cs_["bs"], cs_["nb"], cs_["us"], cs_["vts"]
        mixc = []
        for g in range(4):
            pm = psp.alloc()
            K.mm([(pm[0:64, tb * bs:(tb + 1) * bs], [(vts[tb][0:bs, g * 64:(g + 1) * 64], wTm[0:bs, g, 0:bs])], True, True)
                  for tb in range(nb)])
            t = f32p.alloc()
            K.tt("dve", t[0:64, :n].rearrange("p (r t) -> p r t", r=nb), pm[0:64, :n].rearrange("p (r t) -> p r t", r=nb),
                 biasb[:, g:g + 1, 0:bs].bcast([64, nb, bs]), ALU.add)
            psp.free(pm)
            m = b16p.alloc()
            K.tt("dve", m[0:64, :n], t[0:64, :n], us[g][0:64, :n], ALU.mult)
            f32p.free(t)
            mixc.append(m)
        b16p.free(*us)
        b16p.free(*vts)
        return mixc

    def wout(l, n, mixa, mixb, mixc):
        ssq = psp.alloc()
        ys = []
        for m in range(8):
            wa = ws.get([(wview(w_out[l], 0, 4, m * 128, 128), 128, 4, 0, 128, None)], ("oa", l, m))
            wb = ws.get([(wview(w_out[l], 512, 8, m * 128, 128, p=64), 64, 8, 0, 128, None)], ("ob", l, m))
            py = psp.alloc()
            pairs = [(wa[:, h, :], mixa[h][:, :n]) for h in range(4)]
            pairs += [(wb[0:64, 4 + g, :], mixc[g][0:64, :n]) for g in range(4)]
            pairs += [(wb[0:64, h, :], mixb[h][0:64, :n]) for h in range(4)]
            K.mm1(py[:, :n], pairs)
            evac_y(py, ys, ssq, m, n)
        b16p.free(*mixa)
        b16p.free(*mixb)
        b16p.free(*mixc)
        out_proj_epilogue(ys, ssq, l, 3, n, 1.0)

    def load_x_tm(src, r0, n):
        for tb in range(n // 128):
            xa = [f32p.alloc(), f32p.alloc()]
            K.dma(xa[0][:, :], src[r0 + tb * 128:r0 + (tb + 1) * 128, 0:512])
            K.dma(xa[1][:, :], src[r0 + tb * 128:r0 + (tb + 1) * 128, 512:1024])
            for kg in range(2):
                pt = psp.alloc()
                K.transposes([(pt[:, kk * 128:(kk + 1) * 128], xa[kg][:, kk * 128:(kk + 1) * 128], ident) for kk in range(4)])
                for kk in range(4):
                    K.copy("act" if kk % 2 == 0 else "dve", xs[4 * kg + kk][:, tb * 128:(tb + 1) * 128], pt[:, kk * 128:(kk + 1) * 128])
                psp.free(pt)
            f32p.free(*xa)

    def store_x_tm(dst, r0, n):
        for tb in range(n // 128):
            for kg in range(2):
                pt = psp.alloc()
                K.transposes([(pt[:, kk * 128:(kk + 1) * 128], xs[4 * kg + kk][:, tb * 128:(tb + 1) * 128], ident) for kk in range(4)])
                st = f32p.alloc()
                K.copy("act" if kg == 0 else "dve", st[:, :], pt[:, :])
                psp.free(pt)
                K.dma(dst[r0 + tb * 128:r0 + (tb + 1) * 128, kg * 512:(kg + 1) * 512], st[:, :])
                f32p.free(st)

    def layer_setup(l):
        for h in range(4):
            st = f32p.alloc()
            K.dma(st[:, 0:256].rearrange("p (cc n) -> p cc n", cc=2), V(w_uk, w_uk.t[l, :, h, :].rearrange("(cc p) n -> p cc n", p=128)))
            pt = psp.alloc()
            K.transposes([(pt[:, cc * 128:(cc + 1) * 128], st[:, cc * 128:(cc + 1) * 128], ident) for cc in range(2)])
            K.copy("act", wukT[:, h, :], pt[:, 0:256])
            psp.free(pt)
            f32p.free(st)
        for cc in range(2):
            st = f32p.alloc()
            K.dma(st[:, 0:512], V(w_uv, w_uv.t[l, cc * 128:(cc + 1) * 128, :, :].rearrange("p h v -> p (h v)")))
            K.copy("pool", wuv[:, cc, :], st[:, 0:512])
            f32p.free(st)
        for g in range(4):
            st = f32p.alloc()
            K.dma(st[:, 0:128], cws[l, g, :, :])
            K.tt("pool", st[:, 0:128], st[:, 0:128], tril, ALU.mult)
            pt = psp.alloc()
            K.transposes([(pt[:, 0:128], st[:, 0:128], ident)])
            K.copy("act", wTm[:, g, :], pt[:, 0:128])
            psp.free(pt)
            f32p.free(st)
            K.dma(biasb[:, g, :], V(cbs, cbs.t[l, g:g + 1, :].broadcast_to([64, 128])))
        K.dma(vgn[:, :], V(cvg, cvg.t[l:l + 1, :].broadcast_to([128, 256])))

    def global_setup():
        import os
        lvl = int(os.environ.get("GS", "9"))
        K.dma(cst[:, :], cstd[:, :])
        if lvl < 1:
            return
        K.memset("dve", onesb[:, :], 1.0)
        K.memset("dve", zerob[:, :], 0.0)
        K.memset("dve", epsc[:, :], EPS)
        K.copy("dve", identb[:, :], cst[:, 0:128])
        if lvl < 2:
            return
        K.memset("pool", kT[64:65, 2, :], 1.0)
        if lvl < 3:
            return
        st = f32p.alloc()
        K.memset("dve", st[:, 0:128], 0.0)
        K.dma(st[0:96, 0:128], V(norm_g, norm_g.t.rearrange("l i (k p) -> (l i k) p", p=128)))
        K.dma(st[96:102, 0:128], V(qa_g, qa_g.t.rearrange("l (k p) -> (l k) p", p=128)))
        K.dma(st[102:106, 0:128], V(kva_g, kva_g.t.rearrange("l (k p) -> (l k) p", p=128)))
        K.dma(st[106:114, 0:64], V(hog, hog.t.rearrange("l (h p) -> (l h) p", p=64)))
        K.dma(st[114:122, 0:64], V(lb_logits, lb_logits.t.rearrange("l (h p) -> (l h) p", p=64)))
        if lvl < 4:
            f32p.free(st)
            return
        pt = psp.alloc()
        K.transposes([(pt[:, 0:128], st[:, 0:128], ident)])
        K.copy("act", colv[:, :], pt[:, 0:128])
        psp.free(pt)
        K.ts("dve", colvh[:, :], colv[:, :], 0.5, None, ALU.mult)
        f32p.free(st)
        if lvl < 5:
            return
        K.memset("dve", lbc[:, 0:4], 0.0)
        K.memset("dve", lbc[:, 8:12], 1.0)
        d = f32p.alloc()
        K.tt("dve", d[0:64, 0:4], colv[0:64, R_LB + 4:R_LB + 8], colv[0:64, R_LB:R_LB + 4], ALU.subtract)
        K.act(lbc[:, 4:8], d[0:64, 0:4], AF.Sigmoid)
        K.act(lbc[:, 12:16], d[0:64, 0:4], AF.Sigmoid, scale=-1.0)
        f32p.free(d)

    cfg = build_program.cfg

    def drain(g):
        while True:
            try:
                next(g)
            except StopIteration as e:
                return e.value

    def run_interleaved(g1, n1, g2, n2):
        i1 = i2 = 0
        d1 = d2 = False
        v1 = v2 = None
        while not (d1 and d2):
            if (not d1) and (d2 or i1 * n2 <= i2 * n1):
                try:
                    next(g1)
                    i1 += 1
                except StopIteration as e:
                    d1 = True
                    v1 = e.value
            else:
                try:
                    next(g2)
                    i2 += 1
                except StopIteration as e:
                    d2 = True
                    v2 = e.value
        return v1, v2

    def emit():
        global_setup()
        for l in range(cfg["layers"]):
            if cfg.get("lsetup", True):
                layer_setup(l)
            for ti in cfg["tiles"]:
                sample = ti == NPT
                n = SN if sample else TN
                if cfg.get("noload"):
                    pass
                elif l == 0:
                    load_x_tm(xsm if sample else xp, 0 if sample else ti * TN, n)
                else:
                    for k in range(8):
                        K.dma(xs[k][:, :n], xscr[ti][:, k, :n])
                if "ffn1" in cfg["stages"]:
                    ffn(l, 0, n)
                if "mix" in cfg["stages"]:
                    prenorm(l, 2, n)
                    cst_ = cmlp_a(l, ti, n, sample)
                    st = hgrn_prep(l, ti, n, sample)
                    cmlp_b(cst_)
                    if sample:
                        drain(hgrn_elem_gen(st))
                        hgrn_kdT(st)
                        drain(hgrn_chain_gen(st))
                        mixb = hgrn_fin(st)
                        mixc = cmlp_c(cst_)
                        mixa, agen, nblk = drain(mla(l, ti, n, sample))
                    else:
                        (mixa, agen, nblk), _ = run_interleaved(mla(l, ti, n, sample), 12, hgrn_elem_gen(st), 8)
                        mixc = cmlp_c(cst_)
                        hgrn_kdT(st)
                        run_interleaved(agen, nblk, hgrn_chain_gen(st), st["nch"])
                        mixb = hgrn_fin(st)
                    wout(l, n, mixa, mixb, mixc)
                if "ffn2" in cfg["stages"]:
                    ffn(l, 1, n)
                if cfg.get("nostore"):
                    pass
                elif l == cfg["layers"] - 1:
                    store_x_tm(ysm if sample else yp, 0 if sample else ti * TN, n)
                else:
                    for k in range(8):
                        K.dma(xscr[ti][:, k, :n], xs[k][:, :n])
        K.finish()

    K.dry = True
    emit()
    K.dry = False
    ws.reset()
    f32p.free_list = list(f32p.free_list)
    emit()
    build_program.stats = dict(cnt=dict(K.cnt), f32_low=f32p.low, b16_low=b16p.low, ps_low=psp.low, nws=len(ws.specs))

    with nc.Block() as blk:
        @blk.tensor
        def _(h):
            for f in K.prog["pe"]:
                f(h)

        @blk.scalar
        def _(h):
            for f in K.prog["act"]:
                f(h)

        @blk.vector
        def _(h):
            for f in K.prog["dve"]:
                f(h)

        @blk.gpsimd
        def _(h):
            for f in K.prog["pool"]:
                f(h)

        @blk.sync
        def _(h):
            for f in K.prog["sp"]:
                f(h)
    return nc

build_program.cfg = dict(layers=2, tiles=list(range(NPT + 1)), stages=("ffn1", "mix", "ffn2"))

def host_consts():
    cst = np.zeros((128, 1024), np.float32)
    cst[:, 0:128] = np.eye(128, dtype=np.float32)
    cst[:, 128:256] = np.tril(np.ones((128, 128), np.float32))
    tri = np.triu(np.ones((64, 64), np.float32))
    cst[0:64, 256:512] = np.tile(tri, (1, 4))
    rm = np.ones((128, 512), np.float32)
    rm[:, 0::64] = 0.0
    cst[:, 512:1024] = rm
    pos = np.concatenate([np.arange(SEQ), np.tile(PAST + np.arange(64), NSEQ)]).astype(np.float32)
    inv = (10000.0 ** (-np.arange(0, 64, 2, dtype=np.float32) / 64)).astype(np.float32)
    ang = pos[None, :] * inv[:, None]
    rope = np.zeros((64, 2, SEQ + SN), np.float32)
    rope[0:32, 0] = np.cos(ang)
    rope[32:64, 0] = np.cos(ang)
    rope[0:32, 1] = np.sin(ang)
    rope[32:64, 1] = np.sin(ang)
    return cst, rope


def kernel(x_prompt, x_sample, cache_mla_ckv, cache_mla_kpe, state_hgrn, norm_g, ffn_w_gate, ffn_w_up, ffn_w_down,
           w_in, w_out, mla_qa_g, mla_wqb, mla_kva_g, mla_w_uk, mla_w_uv, hgrn_lb_logits, hgrn_out_g, cmlp_v_g,
           cmlp_w_s, cmlp_b_s):
    f = lambda a: np.ascontiguousarray(np.asarray(a, dtype=np.float32))
    nc = build_program()
    cst, rope = host_consts()
    shared = dict(norm_g=f(norm_g), ffn_w_gate=f(ffn_w_gate), ffn_w_up=f(ffn_w_up), ffn_w_down=f(ffn_w_down),
                  w_in=f(w_in), w_out=f(w_out), mla_qa_g=f(mla_qa_g), mla_wqb=f(mla_wqb), mla_kva_g=f(mla_kva_g),
                  mla_w_uk=f(mla_w_uk), mla_w_uv=f(mla_w_uv), hgrn_lb_logits=f(hgrn_lb_logits), hgrn_out_g=f(hgrn_out_g),
                  cmlp_v_g=f(cmlp_v_g), cmlp_w_s=f(cmlp_w_s), cmlp_b_s=f(cmlp_b_s), cst=cst, rope=rope)
    x_prompt = f(x_prompt)
    x_sample = f(x_sample)
    cache_mla_ckv = f(cache_mla_ckv)
    cache_mla_kpe = f(cache_mla_kpe)
    state_hgrn = f(state_hgrn)
    in_maps = []
    for c in range(8):
        b = c % 4
        s0 = c * NSEQ
        m = dict(shared)
        m["xp"] = x_prompt[b]
        m["xsm"] = x_sample[s0:s0 + NSEQ].reshape(SN, D)
        m["ckv"] = np.ascontiguousarray(cache_mla_ckv[:, s0:s0 + NSEQ])
        m["ckpe"] = np.ascontiguousarray(cache_mla_kpe[:, s0:s0 + NSEQ])
        m["shg"] = np.ascontiguousarray(state_hgrn[:, s0:s0 + NSEQ])
        in_maps.append(m)
    res = run_bass_kernel_spmd(nc, in_maps, core_ids=list(range(8)))
    R = res.results
    y_prompt = np.stack([R[b]["yp"] for b in range(4)], 0)
    y_sample = np.concatenate([R[c]["ysm"].reshape(NSEQ, 64, D) for c in range(8)], 0)
    ckv_p = np.stack([R[b]["ckv_p"] for b in range(4)], 1)
    kpe_p = np.stack([R[b]["kpe_p"] for b in range(4)], 1)
    hg_p = np.stack([R[b]["hg_p"] for b in range(4)], 1)
    ckv_s = np.concatenate([R[c]["ckv_s"].reshape(DEPTH, NSEQ, 64, 256) for c in range(8)], 1)
    kpe_s = np.concatenate([R[c]["kpe_s"].reshape(DEPTH, NSEQ, 64, 64) for c in range(8)], 1)
    hg_s = np.concatenate([R[c]["hg_s"] for c in range(8)], 1)
    cv_s = np.concatenate([R[c]["cv_s"].reshape(DEPTH, NSEQ, 64, 256) for c in range(8)], 1)
    return (y_prompt, y_sample, ckv_p, kpe_p, hg_p, ckv_s, kpe_s, hg_s, cv_s)
```

```python
import numpy as np
import concourse.bass as bass
import concourse.mybir as mybir
from concourse.bass_utils import run_bass_kernel_spmd

F32 = mybir.dt.float32
BF16 = mybir.dt.bfloat16
AF = mybir.ActivationFunctionType
ALU = mybir.AluOpType
AX = mybir.AxisListType

D = 1024
DFF = 2816
NJ = DFF // 128
SEQ = 4096
NPT = 8
TN = 512
SN = 256
NSEQ = 4
PAST = 2048
EPS = 1e-6
MLA_SCALE = 192 ** -0.5
DEPTH = 2
EPOCH = 8000
NDS = 40
O_QA, O_KV, O_KPE, O_HQ, O_HF, O_HI, O_HG, O_CU, O_CV = 0, 384, 640, 704, 960, 1216, 1472, 1728, 1984
R_NG, R_QAG, R_KVG, R_HOG, R_LB = 0, 96, 102, 106, 114

class V:
    def __init__(self, buf, ap):
        self.buf = buf
        self.ap = ap

    def __getitem__(self, idx):
        return V(self.buf, self.ap[idx])

    def bitcast(self, dt):
        return V(self.buf, self.ap.bitcast(dt))

    def rearrange(self, s, **kw):
        return V(self.buf, self.ap.rearrange(s, **kw))

    def bcast(self, shape):
        return V(self.buf, self.ap.broadcast_to(shape))

class Buf:
    def __init__(self, t, track=True):
        self.t = t
        self.w = None
        self.r = {}
        self.track = track
        self.psum = False

    def __getitem__(self, idx):
        return V(self, self.t[idx])

class Pool:
    def __init__(self, bufs, name):
        self.free_list = list(bufs)
        self.name = name
        self.n = len(bufs)
        self.low = len(bufs)

    def alloc(self):
        assert self.free_list, "pool %s exhausted" % self.name
        b = self.free_list.pop(0)
        self.low = min(self.low, len(self.free_list))
        return b

    def free(self, *bs):
        for b in bs:
            assert b not in self.free_list
            self.free_list.append(b)

class Ctx:
    ENG = ("pe", "act", "dve", "pool", "sp")

    def __init__(self, nc):
        self.nc = nc
        self.prog = {e: [] for e in self.ENG}
        self.cnt = {e: 0 for e in self.ENG}
        self.esem = {}
        self.seen = {e: {} for e in self.ENG}
        self.dsems = [nc.alloc_semaphore("dq%d" % i) for i in range(NDS)]
        self.dcnt = [0] * NDS
        self.dnext = 0
        self.dry = False
        self.nbuf = 0

    def sb(self, shape, dt, name=None):
        self.nbuf += 1
        return Buf(self.nc.alloc_sbuf_tensor("sb_" + (name or ("%d" % self.nbuf)), list(shape), dt))

    def psb(self, shape, dt, name=None):
        self.nbuf += 1
        b = Buf(self.nc.alloc_psum_tensor("ps_" + (name or ("%d" % self.nbuf)), list(shape), dt))
        b.psum = True
        return b

    def _semh(self, key):
        if key[0] == "d":
            return self.dsems[key[1]]
        if key not in self.esem:
            self.esem[key] = self.nc.alloc_semaphore("s_%s_%d" % (key[1], key[2]))
        return self.esem[key]

    def _wait(self, eng, tok):
        if tok is None:
            return
        key, val = tok
        if key[0] == "e" and key[1] == "pe" and eng == "pe":
            return
        if self.seen[eng].get(key, 0) >= val:
            return
        self.seen[eng][key] = val
        sem = self._semh(key)
        self.prog[eng].append(lambda h, sem=sem, val=val: h.wait_ge(sem, val))

    def _deps(self, eng, reads, writes, self_raw_only=True):
        for b in reads:
            self._wait(eng, b.w)
        for b in writes:
            if b.w is not None and not (b.w[0][0] == "e" and b.w[0][1] == eng):
                self._wait(eng, b.w)
            for k, tok in b.r.items():
                if k[0] == "e" and k[1] == eng:
                    continue
                self._wait(eng, tok)

    def op(self, eng, fn, reads=(), writes=()):
        if self.dry:
            return
        writes = [b for b in writes if b is not None] + [b for b in reads if b is not None and b.psum]
        reads = [b for b in reads if b is not None and not b.psum]
        self._deps(eng, reads, writes)
        self.cnt[eng] += 1
        c = self.cnt[eng]
        key = ("e", eng, (c - 1) // EPOCH)
        val = (c - 1) % EPOCH + 1
        sem = self._semh(key)
        self.prog[eng].append(lambda h, fn=fn, sem=sem: fn(h).then_inc(sem, 1))
        tok = (key, val)
        for b in reads:
            b.r[key] = tok
        for b in writes:
            b.w = tok
            b.r = {}

    def dma(self, out, in_, queue="sp"):
        if self.dry:
            return
        reads = [in_.buf] if in_.buf.track else []
        writes = [out.buf] if out.buf.track else []
        self._deps(queue, reads, writes)
        k = self.dnext
        self.dnext = (k + 1) % NDS
        if self.dcnt[k] > 0:
            self._wait(queue, (("d", k), self.dcnt[k]))
        self.dcnt[k] += 16
        v = self.dcnt[k]
        sem = self.dsems[k]
        oap, iap = out.ap, in_.ap
        self.prog[queue].append(lambda h, oap=oap, iap=iap, sem=sem: h.dma_start(out=oap, in_=iap).then_inc(sem, 16))
        tok = (("d", k), v)
        for b in reads:
            b.r[("d", k)] = tok
        for b in writes:
            b.w = tok
            b.r = {}

    def finish(self):
        for k in range(NDS):
            if self.dcnt[k] > 0:
                self._wait("sp", (("d", k), self.dcnt[k]))

    def mm(self, groups):
        if self.dry:
            return
        reads, writes = [], []
        for out, pairs, st, sp in groups:
            writes.append(out.buf)
            for a, b in pairs:
                reads += [a.buf, b.buf]

        def fn(h, groups=groups):
            last = None
            for out, pairs, st, sp in groups:
                n = len(pairs)
                for i, (a, b) in enumerate(pairs):
                    last = h.matmul(out.ap, lhsT=a.ap, rhs=b.ap, start=(st and i == 0), stop=(sp and i == n - 1))
            return last
        self.op("pe", fn, reads, writes)

    def mm1(self, out, pairs, start=True, stop=True):
        self.mm([(out, pairs, start, stop)])

    def transposes(self, items):
        if self.dry:
            return
        reads, writes = [], []
        for o, i, idn in items:
            writes.append(o.buf)
            reads += [i.buf, idn.buf]

        def fn(h, items=items):
            last = None
            for o, i, idn in items:
                last = h.transpose(o.ap, i.ap, idn.ap)
            return last
        self.op("pe", fn, reads, writes)

    def act(self, out, in_, func, scale=None, bias=None, eng="act"):
        kw = {}
        reads = [in_.buf]
        if scale is not None:
            if isinstance(scale, V):
                kw["scale"] = scale.ap
                reads.append(scale.buf)
            else:
                kw["scale"] = float(scale)
        if bias is not None:
            if isinstance(bias, V):
                kw["bias"] = bias.ap
                reads.append(bias.buf)
            else:
                kw["bias"] = float(bias)
        self.op("act", lambda h: h.activation(out=out.ap, in_=in_.ap, func=func, **kw), reads, [out.buf])

    def tt(self, eng, out, in0, in1, op):
        self.op(eng, lambda h: h.tensor_tensor(out=out.ap, in0=in0.ap, in1=in1.ap, op=op), [in0.buf, in1.buf], [out.buf])

    def ts(self, eng, out, in0, s1, s2, op0, op1=None):
        reads = [in0.buf]
        a1 = s1
        a2 = s2
        if isinstance(s1, V):
            a1 = s1.ap
            reads.append(s1.buf)
        if isinstance(s2, V):
            a2 = s2.ap
            reads.append(s2.buf)
        if op1 is None:
            self.op(eng, lambda h: h.tensor_scalar(out=out.ap, in0=in0.ap, scalar1=a1, scalar2=None, op0=op0), reads, [out.buf])
        else:
            self.op(eng, lambda h: h.tensor_scalar(out=out.ap, in0=in0.ap, scalar1=a1, scalar2=a2, op0=op0, op1=op1), reads, [out.buf])

    def stt(self, out, in0, scalar, in1, op0, op1):
        reads = [in0.buf, in1.buf]
        sc = scalar
        if isinstance(scalar, V):
            sc = scalar.ap
            reads.append(scalar.buf)
        self.op("dve", lambda h: h.scalar_tensor_tensor(out=out.ap, in0=in0.ap, scalar=sc, in1=in1.ap, op0=op0, op1=op1), reads, [out.buf])

    def copy(self, eng, out, in_):
        if eng == "act":
            self.op("act", lambda h: h.copy(out=out.ap, in_=in_.ap), [in_.buf], [out.buf])
        else:
            self.op(eng, lambda h: h.tensor_copy(out=out.ap, in_=in_.ap), [in_.buf], [out.buf])

    def memset(self, eng, out, val):
        self.op(eng, lambda h: h.memset(out.ap, val), [], [out.buf])

    def scan(self, out, d0, d1, initial, op0, op1):
        self.op("dve", lambda h: h.tensor_tensor_scan(out=out.ap, data0=d0.ap, data1=d1.ap, initial=initial, op0=op0, op1=op1),
                [d0.buf, d1.buf], [out.buf])

    def reduce(self, out, in_, op, axis=None):
        axis = axis or AX.X
        self.op("dve", lambda h: h.tensor_reduce(out=out.ap, in_=in_.ap, axis=axis, op=op), [in_.buf], [out.buf])

class WStream:
    def __init__(self, K, nslots=13, nstage=2, look=8):
        self.K = K
        self.slots = [K.sb([128, 8, 128], BF16, "wslot%d" % i) for i in range(nslots)]
        self.f32p = None
        self.specs = []
        self.pos = 0
        self.issued = 0
        self.look = look
        self.scr = {}
        self.done = set()
        self.nstaged = 0
        self.pre = set()

    def _convert(self, parts, dst):
        K = self.K
        for (src, p, nk, c0, ncols, scale) in parts:
            k0 = 0
            while k0 < nk:
                nkh = min(4, nk - k0)
                pg = self.f32p.alloc()
                st = pg[0:p, 0:nkh * ncols].rearrange("p (k c) -> p k c", k=nkh)
                K.dma(st, V(src.buf, src.ap[:, k0:k0 + nkh, :]))
                o = dst[0:p, k0:k0 + nkh, c0:c0 + ncols]
                if scale is None:
                    K.copy("pool", o, st)
                else:
                    K.ts("pool", o, st, float(scale), None, ALU.mult)
                self.f32p.free(pg)
                k0 += nkh

    def _convert_pair(self, wide, nk, dst_a, dst_b, eng_b="pool"):
        K = self.K
        for k0 in range(0, nk, 2):
            pg = self.f32p.alloc()
            st = pg[:, 0:512].rearrange("p (k c) -> p k c", k=2)
            K.dma(st, V(wide.buf, wide.ap[:, k0:k0 + 2, :]))
            K.copy("pool", dst_a[:, k0:k0 + 2, :], st[:, :, 0:128])
            K.copy(eng_b, dst_b[:, k0:k0 + 2, :], st[:, :, 128:256])
            self.f32p.free(pg)

    def reset(self):
        self.pos = 0
        self.issued = 0
        seen = set()
        self.bg = []
        for key, parts, wide in self.specs:
            if key[1] >= 1 and key not in seen:
                seen.add(key)
                self.bg.append((key, parts, wide))
        self.bgpos = 0
        self.bgbuf = [self.K.sb([128, 8, 128], BF16, "wbg%d" % i) for i in range(4)]

    def background(self, n=1):
        K = self.K
        if K.dry:
            return
        while n > 0 and self.bgpos < len(self.bg):
            key, parts, wide = self.bg[self.bgpos]
            self.bgpos += 1
            if key in self.done:
                continue
            n -= 1
            p, nk = parts[0][1], parts[0][2]
            ext = max(c0 + ncols for (_, _, _, c0, ncols, _) in parts)
            buf = self.bgbuf[self.nstaged % len(self.bgbuf)]
            self.nstaged += 1
            if wide is not None and self.bgpos < len(self.bg) and self.bg[self.bgpos][0] not in self.done:
                key2 = self.bg[self.bgpos][0]
                self.bgpos += 1
                n -= 1
                buf2 = self.bgbuf[self.nstaged % len(self.bgbuf)]
                self.nstaged += 1
                self._convert_pair(wide, nk, buf, buf2)
                K.dma(self.scr[key][:, 0:nk, :], buf[:, 0:nk, :])
                K.dma(self.scr[key2][:, 0:nk, :], buf2[:, 0:nk, :])
                self.done.add(key)
                self.done.add(key2)
                continue
            self._convert(parts, buf)
            K.dma(self.scr[key][0:p, 0:nk, 0:ext], buf[0:p, 0:nk, 0:ext])
            self.done.add(key)

    def _issue(self, i):
        K = self.K
        if i in self.pre:
            return
        key, parts, wide = self.specs[i]
        slot = self.slots[i % len(self.slots)]
        p, nk = parts[0][1], parts[0][2]
        ext = max(c0 + ncols for (_, _, _, c0, ncols, _) in parts)
        scr = self.scr[key]
        if key in self.done:
            K.dma(slot[0:p, 0:nk, 0:ext], scr[0:p, 0:nk, 0:ext])
            return
        if wide is not None and i + 1 < len(self.specs) and self.specs[i + 1][0] not in self.done:
            key2 = self.specs[i + 1][0]
            slot2 = self.slots[(i + 1) % len(self.slots)]
            self._convert_pair(wide, nk, slot, slot2, eng_b="act")
            K.dma(scr[:, 0:nk, :], slot[:, 0:nk, :])
            K.dma(self.scr[key2][:, 0:nk, :], slot2[:, 0:nk, :])
            self.done.add(key)
            self.done.add(key2)
            self.pre.add(i + 1)
            return
        self._convert(parts, slot)
        K.dma(scr[0:p, 0:nk, 0:ext], slot[0:p, 0:nk, 0:ext])
        self.done.add(key)

    def get(self, parts, key, wide=None):
        if self.K.dry:
            self.specs.append((key, parts, wide))
            if key not in self.scr:
                self.scr[key] = Buf(self.K.nc.dram_tensor("wscr%d" % len(self.scr), [128, 8, 128], BF16).ap())
            return self.slots[0]
        i = self.pos
        self.pos += 1
        while self.issued < min(len(self.specs), i + 1 + self.look):
            self._issue(self.issued)
            self.issued += 1
        return self.slots[i % len(self.slots)]

def wview(dv, r0, nk, c0, ncols, p=128):
    return V(dv.buf, dv.ap[r0:r0 + nk * p, c0:c0 + ncols].rearrange("(k p) c -> p k c", p=p))

def build_program(debug=None):
    nc = bass.Bass("TRN2", target_bir_lowering=False)
    K = Ctx(nc)

    def din(name, shape):
        return Buf(nc.dram_tensor(name, list(shape), F32, kind="ExternalInput").ap(), track=False)

    def dout(name, shape):
        return Buf(nc.dram_tensor(name, list(shape), F32, kind="ExternalOutput").ap(), track=False)

    xp = din("xp", [SEQ, D])
    xsm = din("xsm", [SN, D])
    ckv = din("ckv", [DEPTH, NSEQ, PAST, 256])
    ckpe = din("ckpe", [DEPTH, NSEQ, PAST, 64])
    shg = din("shg", [DEPTH, NSEQ, 4, 64, 64])
    norm_g = din("norm_g", [DEPTH, 6, D])
    wgate = din("ffn_w_gate", [DEPTH, 2, D, DFF])
    wup = din("ffn_w_up", [DEPTH, 2, D, DFF])
    wdown = din("ffn_w_down", [DEPTH, 2, DFF, D])
    w_in = din("w_in", [DEPTH, D, 2240])
    w_out = din("w_out", [DEPTH, D, D])
    qa_g = din("mla_qa_g", [DEPTH, 384])
    wqb = din("mla_wqb", [DEPTH, 384, 768])
    kva_g = din("mla_kva_g", [DEPTH, 256])
    w_uk = din("mla_w_uk", [DEPTH, 256, 4, 128])
    w_uv = din("mla_w_uv", [DEPTH, 256, 4, 128])
    lb_logits = din("hgrn_lb_logits", [DEPTH, 256])
    hog = din("hgrn_out_g", [DEPTH, 256])
    cvg = din("cmlp_v_g", [DEPTH, 256])
    cws = din("cmlp_w_s", [DEPTH, 4, 128, 128])
    cbs = din("cmlp_b_s", [DEPTH, 4, 128])
    cstd = din("cst", [128, 1024])
    roped = din("rope", [64, 2, SEQ + SN])

    yp = dout("yp", [SEQ, D])
    ysm = dout("ysm", [SN, D])
    ckv_p = dout("ckv_p", [DEPTH, SEQ, 256])
    kpe_p = dout("kpe_p", [DEPTH, SEQ, 64])
    hg_p = dout("hg_p", [DEPTH, 4, 64, 64])
    ckv_s = dout("ckv_s", [DEPTH, SN, 256])
    kpe_s = dout("kpe_s", [DEPTH, SN, 64])
    hg_s = dout("hg_s", [DEPTH, NSEQ, 4, 64, 64])
    cv_s = dout("cv_s", [DEPTH, SN, 256])
    xscr = [Buf(nc.dram_tensor("xscr%d" % t, [128, 8, TN], F32).ap()) for t in range(NPT + 1)]
    dbg = {}
    if debug:
        for name, shape in debug.items():
            dbg[name] = dout("dbg_" + name, shape)

    xs = [K.sb([128, TN], F32, "xs%d" % k) for k in range(8)]
    hb = [K.sb([128, TN], BF16, "hb%d" % k) for k in range(8)]
    kT = K.sb([128, 3, SEQ], BF16, "kT")
    ctok = K.sb([128, SEQ // 128, 256], BF16, "ctok")
    hit = K.sb([64, 8, 256], BF16, "hit")
    kdt = K.sb([64, 8, 256], BF16, "kdt")
    cst = K.sb([128, 1024], F32, "cst")
    colv = K.sb([128, 128], F32, "colv")
    colvh = K.sb([128, 128], F32, "colvh")
    onesb = K.sb([128, 128], BF16, "onesb")
    identb = K.sb([128, 128], BF16, "identb")
    zerob = K.sb([128, 128], BF16, "zerob")
    epsc = K.sb([128, 1], F32, "epsc")
    wukT = K.sb([128, 4, 256], BF16, "wukT")
    wuv = K.sb([128, 2, 512], BF16, "wuv")
    wTm = K.sb([128, 4, 128], BF16, "wTm")
    biasb = K.sb([64, 4, 128], F32, "biasb")
    vgn = K.sb([128, 256], F32, "vgn")
    lbc = K.sb([64, 16], F32, "lbc")
    S32 = K.sb([64, 4, 64], F32, "S32")
    Sbf = K.sb([64, 4, 64], BF16, "Sbf")
    kmx = K.sb([128, 2], F32, "kmx")
    NF32, NB16 = 20, 46
    f32p = Pool([K.sb([128, TN], F32, "f%d" % i) for i in range(NF32)], "f32")
    b16p = Pool([K.sb([128, TN], BF16, "b%d" % i) for i in range(NB16)], "b16")
    psp = Pool([K.psb([128, TN], F32, "psum%d" % i) for i in range(8)], "psum")
    ws = WStream(K)
    ws.f32p = f32p

    ident = cst[:, 0:128]
    tril = cst[:, 128:256]
    triu4 = cst[0:64, 256:512]
    rmask = cst[:, 512:1024]

    def gcol(l, i, k):
        r = R_NG + (l * 6 + i) * 8 + k
        return colv[:, r:r + 1]

    def gcolh(l, i, k):
        r = R_NG + (l * 6 + i) * 8 + k
        return colvh[:, r:r + 1]

    def norm_stats(src, nk, n, inv_count, P=128):
        ps = psp.alloc()
        for k in range(nk):
            sq = b16p.alloc()
            K.act(sq[0:P, :n], src(k), AF.Square)
            K.mm1(ps[0:P, :n], [(onesb[0:P, 0:P], sq[0:P, :n])], start=(k == 0), stop=(k == nk - 1))
            b16p.free(sq)
        r = f32p.alloc()
        K.act(r[0:P, :n], ps[0:P, :n], AF.Ln, scale=inv_count, bias=epsc[0:P, 0:1])
        psp.free(ps)
        K.act(r[0:P, :n], r[0:P, :n], AF.Exp, scale=-0.5)
        return r

    def prenorm(l, i, n):
        r = norm_stats(lambda k: xs[k][:, :n], 8, n, 1.0 / D)
        for k in range(8):
            K.stt(hb[k][:, :n], xs[k][:, :n], gcol(l, i, k), r[:, :n], ALU.mult, ALU.mult)
        f32p.free(r)

    def out_proj_epilogue(ys, ssq, l, i, n, half):
        flush_ssq()
        r = f32p.alloc()
        K.act(r[:, :n], ssq[:, :n], AF.Ln, scale=1.0 / D, bias=epsc[:, 0:1])
        psp.free(ssq)
        K.act(r[:, :n], r[:, :n], AF.Exp, scale=-0.5)
        for m in range(8):
            t = f32p.alloc()
            K.stt(t[:, :n], ys[m][:, :n], gcol(l, i, m), r[:, :n], ALU.mult, ALU.mult)
            K.stt(xs[m][:, :n], t[:, :n], float(half), xs[m][:, :n], ALU.mult, ALU.add)
            f32p.free(t)
            f32p.free(ys[m])
        f32p.free(r)

    pend_ssq = []

    def flush_ssq():
        while pend_ssq:
            pend_ssq.pop(0)()

    def evac_y(py, ys, ssq, m, n):
        flush_ssq()
        y = f32p.alloc()
        ys.append(y)
        sq = b16p.alloc()
        K.act(sq[:, :n], py[:, :n], AF.Square)
        K.copy("dve", y[:, :n], py[:, :n])
        psp.free(py)

        def later():
            K.mm1(ssq[:, :n], [(onesb[:, :], sq[:, :n])], start=(m == 0), stop=(m == 7))
            b16p.free(sq)
        pend_ssq.append(later)

    def ffn(l, fi, n):
        ni_pre, ni_post = (0, 1) if fi == 0 else (4, 5)
        prenorm(l, ni_pre, n)
        gs = []
        for j0 in range(0, NJ, 2):
            wg, wu = [], []
            for jj in (j0, j0 + 1):
                wg.append(ws.get([(wview(wgate[l, fi], 0, 8, jj * 128, 128), 128, 8, 0, 128, None)], ("g", l, fi, jj),
                                 wide=(wview(wgate[l, fi], 0, 8, j0 * 128, 256) if jj == j0 else None)))
            for jj in (j0, j0 + 1):
                wu.append(ws.get([(wview(wup[l, fi], 0, 8, jj * 128, 128), 128, 8, 0, 128, None)], ("u", l, fi, jj),
                                 wide=(wview(wup[l, fi], 0, 8, j0 * 128, 256) if jj == j0 else None)))
            for q in range(2):
                pg = psp.alloc()
                pu = psp.alloc()
                if j0 == 0 and q == 0:
                    for k in range(8):
                        K.mm1(pg[:, :n], [(wg[q][:, k, :], hb[k][:, :n])], start=(k == 0), stop=(k == 7))
                else:
                    K.mm1(pg[:, :n], [(wg[q][:, k, :], hb[k][:, :n]) for k in range(8)])
                K.mm1(pu[:, :n], [(wu[q][:, k, :], hb[k][:, :n]) for k in range(8)])
                sg = f32p.alloc()
                K.act(sg[:, :n], pg[:, :n], AF.Silu)
                psp.free(pg)
                gj = b16p.alloc()
                K.tt("dve", gj[:, :n], sg[:, :n], pu[:, :n], ALU.mult)
                psp.free(pu)
                f32p.free(sg)
                gs.append(gj)
                if l + 1 < cfg["layers"] and q == 1:
                    ws.background(1)
        ssq = psp.alloc()
        ys = []
        for m0 in range(0, 8, 2):
            py = [psp.alloc(), psp.alloc()]
            for pi, (r0, nk) in enumerate(((0, 8), (1024, 8), (2048, 6))):
                w = []
                for q in range(2):
                    m = m0 + q
                    w.append(ws.get([(wview(wdown[l, fi], r0, nk, m * 128, 128), 128, nk, 0, 128, None)], ("d", l, fi, m, r0),
                                    wide=(wview(wdown[l, fi], r0, nk, m0 * 128, 256) if q == 0 else None)))
                for q in range(2):
                    K.mm1(py[q][:, :n], [(w[q][:, k, :], gs[pi * 8 + k][:, :n]) for k in range(nk)], start=(pi == 0), stop=(pi == 2))
            for q in range(2):
                evac_y(py[q], ys, ssq, m0 + q, n)
        b16p.free(*gs)
        out_proj_epilogue(ys, ssq, l, ni_post, n, 0.5)

    def win_slot(l, c0, ncols):
        return ws.get([(wview(w_in[l], 0, 8, c0, ncols), 128, 8, 0, ncols, None)], ("in", l, c0))

    def win_rot_slot(l, c0):
        return ws.get([(wview(w_in[l], 0, 8, c0 + 32, 32), 128, 8, 0, 32, -1.0),
                       (wview(w_in[l], 0, 8, c0, 32), 128, 8, 32, 32, None)], ("inrot", l, c0))

    def proj_fm(wslot, nk, M, n, src=None, srcP=128):
        ps = psp.alloc()
        if src is None:
            src = lambda k: hb[k][:, :n]
        K.mm1(ps[0:M, :n], [(wslot[0:srcP, k, 0:M], src(k)) for k in range(nk)])
        return ps

    def rope_from_psum(pp, ppr, cs, n, out):
        t1 = f32p.alloc()
        t2 = f32p.alloc()
        K.tt("dve", t1[0:64, :n], pp[0:64, :n], cs[0][0:64, :n], ALU.mult)
        psp.free(pp)
        K.tt("dve", t2[0:64, :n], ppr[0:64, :n], cs[1][0:64, :n], ALU.mult)
        psp.free(ppr)
        K.tt("dve", out, t1[0:64, :n], t2[0:64, :n], ALU.add)
        f32p.free(t1, t2)

    def attend_gen(h, qT, n_q, blocks, mixa_dst, pend):
        po = [psp.alloc(), psp.alloc()]
        pd = psp.alloc()
        nb = len(blocks)

        def qk(bi):
            kf, cf, nk, c0, diag = blocks[bi]
            pst = psp.alloc()
            K.mm1(pst[0:nk, c0:n_q], [(kf(0), qT[3 * h][:, c0:n_q]), (kf(1), qT[3 * h + 1][:, c0:n_q]),
                                      (kf(2), qT[3 * h + 2][0:65, c0:n_q])])
            pT = b16p.alloc()
            K.act(pT[0:nk, c0:n_q], pst[0:nk, c0:n_q], AF.Exp, scale=MLA_SCALE)
            psp.free(pst)
            if diag:
                K.memset("pool", pT[64:128, c0:c0 + 64], 0.0)
            return pT

        nxt = qk(0)
        for bi, (kf, cf, nk, c0, diag) in enumerate(blocks):
            pT = nxt
            if bi + 1 < nb:
                nxt = qk(bi + 1)
            first, last = bi == 0, bi == nb - 1
            K.mm([(po[0][:, c0:n_q], [(cf(0), pT[0:nk, c0:n_q])], first, last),
                  (po[1][:, c0:n_q], [(cf(1), pT[0:nk, c0:n_q])], first, last),
                  (pd[:, c0:n_q], [(onesb[0:nk, :], pT[0:nk, c0:n_q])], first, last)])
            b16p.free(pT)
            if bi == 1 and pend:
                pend.pop()()
            yield
        if pend:
            pend.pop()()

        def fin():
            rd = f32p.alloc()
            K.act(rd[:, :n_q], pd[:, :n_q], AF.Ln)
            psp.free(pd)
            K.act(rd[:, :n_q], rd[:, :n_q], AF.Exp, scale=-1.0)
            ol = [b16p.alloc(), b16p.alloc()]
            for cc in range(2):
                K.tt("dve", ol[cc][:, :n_q], po[cc][:, :n_q], rd[:, :n_q], ALU.mult)
                psp.free(po[cc])
            f32p.free(rd)
            pv = psp.alloc()
            K.mm1(pv[:, :n_q], [(wuv[:, cc, h * 128:(h + 1) * 128], ol[cc][:, :n_q]) for cc in range(2)])
            b16p.free(*ol)
            K.copy("act", mixa_dst, pv[:, :n_q])
            psp.free(pv)
        pend.append(fin)

    def key_norm_update(kf, nkeys, first):
        ps = psp.alloc()
        for f in range(3):
            P = 128 if f < 2 else 64
            sq = b16p.alloc()
            K.act(sq[0:P, :nkeys], kf(f)[0:P], AF.Square)
            K.mm1(ps[:, :nkeys], [(onesb[0:P, :], sq[0:P, :nkeys])], start=(f == 0), stop=(f == 2))
            b16p.free(sq)
        t = f32p.alloc()
        K.act(t[:, :nkeys], ps[:, :nkeys], AF.Sqrt)
        psp.free(ps)
        m = f32p.alloc()
        K.reduce(m[:, 0:1], t[:, :nkeys], ALU.max)
        if first:
            K.copy("dve", kmx[:, 0:1], m[:, 0:1])
        else:
            K.tt("dve", kmx[:, 0:1], kmx[:, 0:1], m[:, 0:1], ALU.max)
        K.ts("dve", kmx[:, 1:2], kmx[:, 0:1], -1.0, None, ALU.mult)
        f32p.free(t, m)

    def mla(l, ti, n, sample):
        tok0 = ti * TN
        cs = [f32p.alloc(), f32p.alloc()]
        rc0 = SEQ if sample else tok0
        K.dma(cs[0][0:64, :n], roped[:, 0, rc0:rc0 + n])
        K.dma(cs[1][0:64, :n], roped[:, 1, rc0:rc0 + n])
        pkv = [proj_fm(win_slot(l, O_KV + 128 * c, 128), 8, 128, n) for c in range(2)]
        rkv = norm_stats(lambda k: pkv[k][:, :n], 2, n, 1.0 / 256)
        cn = [f32p.alloc(), f32p.alloc()]
        for c in range(2):
            K.stt(cn[c][:, :n], pkv[c][:, :n], colv[:, R_KVG + l * 2 + c:R_KVG + l * 2 + c + 1], rkv[:, :n], ALU.mult, ALU.mult)
            psp.free(pkv[c])
        f32p.free(rkv)
        yield
        pk = proj_fm(win_slot(l, O_KPE, 64), 8, 64, n)
        pkr = proj_fm(win_rot_slot(l, O_KPE), 8, 64, n)
        kpe = f32p.alloc()
        rope_from_psum(pk, pkr, cs, n, kpe[0:64, :n])
        yield
        if sample:
            kTn = [b16p.alloc() for _ in range(3)]
            kdst = lambda f: kTn[f][:, :n]
            K.memset("pool", kTn[2][64:65, :n], 1.0)
        else:
            kTn = None
            kdst = lambda f: kT[:, f, tok0:tok0 + n]
        K.copy("pool", kdst(0), cn[0][:, :n])
        K.copy("pool", kdst(1), cn[1][:, :n])
        K.copy("pool", kdst(2)[0:64], kpe[0:64, :n])
        if not sample:
            for tb in range(n // 128):
                pt = psp.alloc()
                sl = slice(tb * 128, (tb + 1) * 128)
                K.transposes([(pt[:, 0:128], cn[0][:, sl], ident), (pt[:, 128:256], cn[1][:, sl], ident),
                              (pt[:, 256:320], kpe[0:64, sl], ident[0:64, 0:64])])
                ct = f32p.alloc()
                K.copy("act", ct[:, 0:320], pt[:, 0:320])
                K.copy("dve", ctok[:, ti * 4 + tb, :], pt[:, 0:256])
                psp.free(pt)
                K.dma(ckv_p[l, tok0 + tb * 128:tok0 + (tb + 1) * 128, :], ct[:, 0:256])
                K.dma(kpe_p[l, tok0 + tb * 128:tok0 + (tb + 1) * 128, :], ct[:, 256:320])
                f32p.free(ct)
                yield
            ctn = None
        else:
            ctn = [b16p.alloc() for _ in range(NSEQ // 2)]
            for s in range(NSEQ):
                pt = psp.alloc()
                sl = slice(s * 64, (s + 1) * 64)
                K.transposes([(pt[0:64, 0:128], cn[0][:, sl], ident), (pt[0:64, 128:256], cn[1][:, sl], ident),
                              (pt[0:64, 256:320], kpe[0:64, sl], ident[0:64, 0:64])])
                ct = f32p.alloc()
                K.copy("act", ct[0:64, 0:320], pt[0:64, 0:320])
                K.copy("dve", ctn[s // 2][0:64, (s % 2) * 256:(s % 2) * 256 + 256], pt[0:64, 0:256])
                psp.free(pt)
                K.dma(ckv_s[l, s * 64:(s + 1) * 64, :], ct[0:64, 0:256])
                K.dma(kpe_s[l, s * 64:(s + 1) * 64, :], ct[0:64, 256:320])
                f32p.free(ct)
        f32p.free(cn[0], cn[1], kpe)
        if not sample:
            key_norm_update(lambda f: kT[:, f, tok0:tok0 + n], n, ti == 0)
        yield
        pq = [proj_fm(win_slot(l, O_QA + 128 * c, 128), 8, 128, n) for c in range(3)]
        rq = norm_stats(lambda k: pq[k][:, :n], 3, n, 1.0 / 384)
        qan = [b16p.alloc() for _ in range(3)]
        for c in range(3):
            K.stt(qan[c][:, :n], pq[c][:, :n], colv[:, R_QAG + l * 3 + c:R_QAG + l * 3 + c + 1], rq[:, :n], ALU.mult, ALU.mult)
            psp.free(pq[c])
        f32p.free(rq)
        qT = [b16p.alloc() for _ in range(12)]
        qsrc = lambda k: qan[k][:, :n]
        for h in range(4):
            wn = ws.get([(wview(wqb[l], 0, 3, 192 * h, 128), 128, 3, 0, 128, None)], ("qn", l, h))
            wp = ws.get([(wview(wqb[l], 0, 3, 192 * h + 128, 64), 128, 3, 0, 64, None)], ("qp", l, h))
            wpr = ws.get([(wview(wqb[l], 0, 3, 192 * h + 160, 32), 128, 3, 0, 32, -1.0),
                          (wview(wqb[l], 0, 3, 192 * h + 128, 32), 128, 3, 32, 32, None)], ("qpr", l, h))
            pn = proj_fm(wn, 3, 128, n, qsrc)
            qn = b16p.alloc()
            K.copy("act", qn[:, :n], pn[:, :n])
            psp.free(pn)
            pp = proj_fm(wp, 3, 64, n, qsrc)
            ppr = proj_fm(wpr, 3, 64, n, qsrc)
            rope_from_psum(pp, ppr, cs, n, qT[3 * h + 2][0:64, :n])
            for cc in range(2):
                pl = psp.alloc()
                K.mm1(pl[:, :n], [(wukT[:, h, cc * 128:(cc + 1) * 128], qn[:, :n])])
                K.copy("act", qT[3 * h + cc][:, :n], pl[:, :n])
                psp.free(pl)
            b16p.free(qn)
            yield
        b16p.free(*qan)
        f32p.free(*cs)

        def q_bound(h, c0, c1):
            ps = psp.alloc()
            w = c1 - c0
            for f in range(3):
                P = 128 if f < 2 else 64
                sq = b16p.alloc()
                K.act(sq[0:P, :w], qT[3 * h + f][0:P, c0:c1], AF.Square)
                K.mm1(ps[:, :w], [(onesb[0:P, :], sq[0:P, :w])], start=(f == 0), stop=(f == 2))
                b16p.free(sq)
            t = f32p.alloc()
            K.act(t[:, :w], ps[:, :w], AF.Sqrt)
            psp.free(ps)
            K.ts("dve", qT[3 * h + 2][64:65, c0:c1], t[64:65, :w], kmx[64:65, 1:2], None, ALU.mult)
            f32p.free(t)

        mixa = [b16p.alloc() for _ in range(4)]
        if not sample:
            for h in range(4):
                q_bound(h, 0, n)

            def attn_gen():
                pend = []
                for h in range(4):
                    blocks = []
                    for kb in range(4 * ti + 4):
                        i = kb - 4 * ti
                        c0 = 128 * i if i > 0 else 0
                        ksl = slice(kb * 128, (kb + 1) * 128)
                        blocks.append((lambda f, ksl=ksl: kT[0:(128 if f < 2 else 65), f, ksl],
                                       lambda cc, kb=kb: ctok[:, kb, cc * 128:(cc + 1) * 128], 128, c0, i >= 0))
                    yield from attend_gen(h, qT, n, blocks, mixa[h][:, :n], pend)
                pend.pop()()
                b16p.free(*qT)
            return mixa, attn_gen(), 4 * (4 * ti + 4)
        else:
            for s in range(NSEQ):
                qsl = slice(s * 64, (s + 1) * 64)

                def load_block(kb):
                    kc = f32p.alloc()
                    K.dma(kc[:, 0:256], ckv[l, s, kb * 128:(kb + 1) * 128, :])
                    K.dma(kc[:, 256:320], ckpe[l, s, kb * 128:(kb + 1) * 128, :])
                    pt = psp.alloc()
                    K.transposes([(pt[:, 0:128], kc[:, 0:128], ident), (pt[:, 128:256], kc[:, 128:256], ident),
                                  (pt[0:64, 256:384], kc[:, 256:320], ident)])
                    kts = b16p.alloc()
                    K.copy("act", kts[:, 0:256], pt[:, 0:256])
                    K.copy("act", kts[0:64, 256:384], pt[0:64, 256:384])
                    psp.free(pt)
                    return kc, kts
                key_norm_update(lambda f: kTn[f][0:(128 if f < 2 else 64), qsl], 64, True)
                for kb in range(PAST // 128):
                    kc, kts = load_block(kb)
                    key_norm_update(lambda f, kts=kts: kts[0:(128 if f < 2 else 64), f * 128:(f + 1) * 128], 128, False)
                    f32p.free(kc)
                    b16p.free(kts)
                for h in range(4):
                    q_bound(h, s * 64, (s + 1) * 64)
                po = [psp.alloc(), psp.alloc()]
                pd = psp.alloc()
                K.mm([(po[0][:, 0:256], [(zerob[:, :], qT[0][:, 0:256])], True, False),
                      (po[1][:, 0:256], [(zerob[:, :], qT[0][:, 0:256])], True, False),
                      (pd[:, 0:256], [(zerob[:, :], qT[0][:, 0:256])], True, False)])
                nblk = PAST // 128 + 1
                for kb in range(nblk):
                    if kb < nblk - 1:
                        kc, kts = load_block(kb)
                        K.memset("pool", kts[64:65, 256:384], 1.0)
                        ctk = b16p.alloc()
                        K.copy("pool", ctk[:, 0:256], kc[:, 0:256])
                        f32p.free(kc)
                        kf = lambda f, kts=kts: kts[0:(128 if f < 2 else 65), f * 128:(f + 1) * 128]
                        cf = lambda cc, ctk=ctk: ctk[:, cc * 128:(cc + 1) * 128]
                        nk = 128
                    else:
                        kts = ctk = None
                        kf = lambda f: kTn[f][0:(128 if f < 2 else 65), qsl]
                        cf = lambda cc: ctn[s // 2][0:64, (s % 2) * 256 + cc * 128:(s % 2) * 256 + (cc + 1) * 128]
                        nk = 64
                    pst = psp.alloc()
                    K.mm([(pst[0:nk, h * 64:(h + 1) * 64], [(kf(0), qT[3 * h][:, qsl]), (kf(1), qT[3 * h + 1][:, qsl]),
                                                             (kf(2), qT[3 * h + 2][0:65, qsl])], True, True) for h in range(4)])
                    pT = b16p.alloc()
                    K.act(pT[0:nk, 0:256], pst[0:nk, 0:256], AF.Exp, scale=MLA_SCALE)
                    psp.free(pst)
                    last = kb == nblk - 1
                    K.mm([(po[0][:, 0:256], [(cf(0), pT[0:nk, 0:256])], False, last),
                          (po[1][:, 0:256], [(cf(1), pT[0:nk, 0:256])], False, last),
                          (pd[:, 0:256], [(onesb[0:nk, :], pT[0:nk, 0:256])], False, last)])
                    b16p.free(pT)
                    if kts is not None:
                        b16p.free(kts, ctk)
                rd = f32p.alloc()
                K.act(rd[:, 0:256], pd[:, 0:256], AF.Ln)
                psp.free(pd)
                K.act(rd[:, 0:256], rd[:, 0:256], AF.Exp, scale=-1.0)
                ol = [b16p.alloc(), b16p.alloc()]
                for cc in range(2):
                    K.tt("dve", ol[cc][:, 0:256], po[cc][:, 0:256], rd[:, 0:256], ALU.mult)
                    psp.free(po[cc])
                f32p.free(rd)
                pv = psp.alloc()
                K.mm([(pv[:, h * 64:(h + 1) * 64], [(wuv[:, cc, h * 128:(h + 1) * 128], ol[cc][:, h * 64:(h + 1) * 64]) for cc in range(2)],
                       True, True) for h in range(4)])
                b16p.free(*ol)
                for h in range(4):
                    K.copy("act", mixa[h][:, qsl], pv[:, h * 64:(h + 1) * 64])
                psp.free(pv)
            b16p.free(*kTn)
            b16p.free(*ctn)
        b16p.free(*qT)
        return mixa, None, 0

    def hgrn_prep(l, ti, n, sample):
        nch = n // 64
        hq, fbs = [], []
        for h in range(4):
            p = proj_fm(win_slot(l, O_HQ + 64 * h, 64), 8, 64, n)
            t = f32p.alloc()
            K.copy("act", t[0:64, :n], p[0:64, :n])
            psp.free(p)
            hq.append(t)
        for h in range(4):
            p = proj_fm(win_slot(l, O_HF + 64 * h, 64), 8, 64, n)
            t = f32p.alloc()
            K.act(t[0:64, :n], p[0:64, :n], AF.Sigmoid)
            psp.free(p)
            fbs.append(t)
        w0 = win_slot(l, O_HI, 128)
        w1 = win_slot(l, O_HI + 128, 128)
        for j in range(nch):
            ps = psp.alloc()
            K.mm([(ps[0:64, 0:128], [(hb[k][:, 64 * j:64 * j + 64], w0[:, k, :]) for k in range(8)], True, True),
                  (ps[0:64, 128:256], [(hb[k][:, 64 * j:64 * j + 64], w1[:, k, :]) for k in range(8)], True, True)])
            K.copy("dve" if j % 2 else "act", hit[:, j, :], ps[0:64, 0:256])
            psp.free(ps)
        hgs = []
        for h in range(4):
            p = proj_fm(win_slot(l, O_HG + 64 * h, 64), 8, 64, n)
            t = b16p.alloc()
            K.act(t[0:64, :n], p[0:64, :n], AF.Silu)
            psp.free(p)
            hgs.append(t)
        st = dict(l=l, ti=ti, n=n, sample=sample, nch=nch, hgs=hgs, hq=hq, fbs=fbs)
        return st

    def hgrn_elem_gen(st):
        l, n, nch, hq, fbs = st["l"], st["n"], st["nch"], st["hq"], st["fbs"]
        kk, bb = [], []
        for h in range(4):
            t = fbs[h]
            K.ts("dve", t[0:64, :n], t[0:64, :n], lbc[:, 8 + l * 4 + h:8 + l * 4 + h + 1], lbc[:, l * 4 + h:l * 4 + h + 1], ALU.mult, ALU.add)
            k_ = f32p.alloc()
            K.ts("dve", k_[0:64, :n], t[0:64, :n], -1.0, 1.0, ALU.mult, ALU.add)
            K.act(t[0:64, :n], t[0:64, :n], AF.Ln)
            b_ = f32p.alloc()
            K.scan(b_[0:64, :n], rmask[0:64, :n], t[0:64, :n], 0.0, ALU.mult, ALU.add)
            f32p.free(t)
            kk.append(k_)
            bb.append(b_)
            yield
        eb, qe, ke, kd = [], [], [], []
        for h in range(4):
            e = f32p.alloc()
            K.act(e[0:64, :n], bb[h][0:64, :n], AF.Exp)
            q = b16p.alloc()
            K.tt("dve", q[0:64, :n], hq[h][0:64, :n], e[0:64, :n], ALU.mult)
            K.act(hq[h][0:64, :n], bb[h][0:64, :n], AF.Exp, scale=-1.0)
            K.tt("dve", kk[h][0:64, :n], kk[h][0:64, :n], hq[h][0:64, :n], ALU.mult)
            k2 = b16p.alloc()
            K.copy("pool", k2[0:64, :n], kk[h][0:64, :n])
            d2 = b16p.alloc()
            for j in range(nch):
                K.ts("dve", d2[0:64, 64 * j:64 * j + 64], kk[h][0:64, 64 * j:64 * j + 64], e[0:64, 64 * j + 63:64 * j + 64], None, ALU.mult)
            f32p.free(hq[h], bb[h], kk[h])
            eb.append(e)
            qe.append(q)
            ke.append(k2)
            kd.append(d2)
            yield
        st.update(eb=eb, qe=qe, ke=ke, kd=kd)

    def hgrn_kdT(st):
        nch, kd = st["nch"], st["kd"]
        for j in range(nch):
            ps = psp.alloc()
            pb = ps[0:64, 0:128].bitcast(BF16)
            K.transposes([(pb[:, h * 64:(h + 1) * 64], kd[h][0:64, 64 * j:64 * j + 64], identb[0:64, 0:64]) for h in range(4)])
            K.copy("act", kdt[:, j, :], pb[:, 0:256])
            psp.free(ps)
        b16p.free(*kd)

    def hgrn_chain_gen(st):
        l, ti, n, sample, nch, eb, qe, ke = st["l"], st["ti"], st["n"], st["sample"], st["nch"], st["eb"], st["qe"], st["ke"]
        ohg = [f32p.alloc() for _ in range(4)]
        st["ohg"] = ohg
        if (not sample) and ti == 0:
            K.memset("dve", S32[:, :, :], 0.0)
            K.memset("pool", Sbf[:, :, :], 0.0)
        for j in range(nch):
            jc = slice(64 * j, 64 * j + 64)
            if sample:
                K.dma(S32[:, :, :], shg[l, j].rearrange("h e d -> e h d"))
                K.copy("dve", Sbf[:, :, :], S32[:, :, :])
            pa = psp.alloc()
            K.mm([(pa[0:64, h * 64:(h + 1) * 64], [(ke[h][0:64, jc], qe[h][0:64, jc])], True, True) for h in range(4)])
            atm = b16p.alloc()
            K.tt("dve", atm[0:64, 0:256], pa[0:64, 0:256], triu4, ALU.mult)
            psp.free(pa)
            po = psp.alloc()
            K.mm([(po[0:64, h * 64:(h + 1) * 64], [(Sbf[:, h, :], qe[h][0:64, jc]),
                                                    (hit[:, j, h * 64:(h + 1) * 64], atm[0:64, h * 64:(h + 1) * 64])], True, True)
                  for h in range(4)])
            b16p.free(atm)
            for h in range(4):
                K.copy("act", ohg[h][0:64, jc], po[0:64, h * 64:(h + 1) * 64])
            psp.free(po)
            pn = psp.alloc()
            K.mm([(pn[0:64, h * 64:(h + 1) * 64], [(kdt[:, j, h * 64:(h + 1) * 64], hit[:, j, h * 64:(h + 1) * 64])], True, True)
                  for h in range(4)])
            for h in range(4):
                K.stt(S32[:, h, :], S32[:, h, :], eb[h][0:64, 64 * j + 63:64 * j + 64], pn[0:64, h * 64:(h + 1) * 64], ALU.mult, ALU.add)
            psp.free(pn)
            K.copy("dve", Sbf[:, :, :], S32[:, :, :])
            if sample:
                K.dma(hg_s[l, j].rearrange("h e d -> e h d"), S32[:, :, :])
            yield
        if (not sample) and ti == NPT - 1:
            K.dma(hg_p[l].rearrange("h e d -> e h d"), S32[:, :, :])
        b16p.free(*qe)
        b16p.free(*ke)
        f32p.free(*eb)

    def hgrn_fin(st):
        l, n, ohg, hgs = st["l"], st["n"], st["ohg"], st["hgs"]
        mixb = []
        for h in range(4):
            r = norm_stats(lambda k: ohg[h][0:64, :n], 1, n, 1.0 / 64, P=64)
            K.stt(ohg[h][0:64, :n], ohg[h][0:64, :n], colv[0:64, R_HOG + l * 4 + h:R_HOG + l * 4 + h + 1], r[0:64, :n], ALU.mult, ALU.mult)
            f32p.free(r)
            m = b16p.alloc()
            K.tt("dve", m[0:64, :n], ohg[h][0:64, :n], hgs[h][0:64, :n], ALU.mult)
            mixb.append(m)
        f32p.free(*ohg)
        b16p.free(*hgs)
        return mixb

    def cmlp_a(l, ti, n, sample):
        bs = 64 if sample else 128
        nb = n // bs
        us = []
        for g in range(4):
            p = proj_fm(win_slot(l, O_CU + 64 * g, 64), 8, 64, n)
            t = b16p.alloc()
            K.act(t[0:64, :n], p[0:64, :n], AF.Gelu_apprx_tanh)
            psp.free(p)
            us.append(t)
        w0 = win_slot(l, O_CV, 128)
        w1 = win_slot(l, O_CV + 128, 128)
        vgs = []
        for tb in range(nb):
            tsl = slice(tb * bs, (tb + 1) * bs)
            ps = psp.alloc()
            K.mm([(ps[0:bs, 0:128], [(hb[k][:, tsl], w0[:, k, :]) for k in range(8)], True, True),
                  (ps[0:bs, 128:256], [(hb[k][:, tsl], w1[:, k, :]) for k in range(8)], True, True)])
            vg = f32p.alloc()
            K.act(vg[0:bs, 0:256], ps[0:bs, 0:256], AF.Gelu_apprx_tanh)
            psp.free(ps)
            vgs.append(vg)
        return dict(l=l, n=n, sample=sample, bs=bs, nb=nb, us=us, vgs=vgs)

    def cmlp_b(cs_):
        l, n, sample, bs, nb, us, vgs = cs_["l"], cs_["n"], cs_["sample"], cs_["bs"], cs_["nb"], cs_["us"], cs_["vgs"]
        vts = []
        for tb in range(nb):
            vg = vgs[tb]
            sq = f32p.alloc()
            K.tt("pool", sq[0:bs, 0:256], vg[0:bs, 0:256], vg[0:bs, 0:256], ALU.mult)
            ss = f32p.alloc()
            K.reduce(ss[0:bs, 0:4], sq[0:bs, 0:256].rearrange("p (g d) -> p g d", g=4), ALU.add)
            K.act(ss[0:bs, 0:4], ss[0:bs, 0:4], AF.Ln, scale=1.0 / 64, bias=epsc[0:bs, 0:1])
            K.act(ss[0:bs, 0:4], ss[0:bs, 0:4], AF.Exp, scale=-0.5)
            vv = vg[0:bs, 0:256].rearrange("p (g d) -> p g d", g=4)
            K.tt("dve", vv, vv, ss[0:bs, 0:4].rearrange("p (g o) -> p g o", o=1).bcast([bs, 4, 64]), ALU.mult)
            K.tt("dve", vg[0:bs, 0:256], vg[0:bs, 0:256], vgn[0:bs, :], ALU.mult)
            if sample:
                K.dma(cv_s[l, tb * 64:(tb + 1) * 64, :], vg[0:64, 0:256])
            vt = b16p.alloc()
            K.copy("pool", vt[0:bs, 0:256], vg[0:bs, 0:256])
            f32p.free(vg, sq, ss)
            vts.append(vt)
        cs_["vts"] = vts

    def cmlp_c(cs_):
        l, n, sample, bs, nb, us, vts = cs_["l"], cs_["n"], cs_["sample"], cs_["bs"], cs_["nb"], cs_["us"], cs_["vts"]
        mixc = []
        for g in range(4):
            pm = psp.alloc()
            K.mm([(pm[0:64, tb * bs:(tb + 1) * bs], [(vts[tb][0:bs, g * 64:(g + 1) * 64], wTm[0:bs, g, 0:bs])], True, True)
                  for tb in range(nb)])
            t = f32p.alloc()
            K.tt("dve", t[0:64, :n].rearrange("p (r t) -> p r t", r=nb), pm[0:64, :n].rearrange("p (r t) -> p r t", r=nb),
                 biasb[:, g:g + 1, 0:bs].bcast([64, nb, bs]), ALU.add)
            psp.free(pm)
            m = b16p.alloc()
            K.tt("dve", m[0:64, :n], t[0:64, :n], us[g][0:64, :n], ALU.mult)
            f32p.free(t)
            mixc.append(m)
        b16p.free(*us)
        b16p.free(*vts)
        return mixc

    def wout(l, n, mixa, mixb, mixc):
        ssq = psp.alloc()
        ys = []
        for m in range(8):
            wa = ws.get([(wview(w_out[l], 0, 4, m * 128, 128), 128, 4, 0, 128, None)], ("oa", l, m))
            wb = ws.get([(wview(w_out[l], 512, 8, m * 128, 128, p=64), 64, 8, 0, 128, None)], ("ob", l, m))
            py = psp.alloc()
            pairs = [(wa[:, h, :], mixa[h][:, :n]) for h in range(4)]
            pairs += [(wb[0:64, 4 + g, :], mixc[g][0:64, :n]) for g in range(4)]
            pairs += [(wb[0:64, h, :], mixb[h][0:64, :n]) for h in range(4)]
            K.mm1(py[:, :n], pairs)
            evac_y(py, ys, ssq, m, n)
        b16p.free(*mixa)
        b16p.free(*mixb)
        b16p.free(*mixc)
        out_proj_epilogue(ys, ssq, l, 3, n, 1.0)

    def load_x_tm(src, r0, n):
        for tb in range(n // 128):
            xa = [f32p.alloc(), f32p.alloc()]
            K.dma(xa[0][:, :], src[r0 + tb * 128:r0 + (tb + 1) * 128, 0:512])
            K.dma(xa[1][:, :], src[r0 + tb * 128:r0 + (tb + 1) * 128, 512:1024])
            for kg in range(2):
                pt = psp.alloc()
                K.transposes([(pt[:, kk * 128:(kk + 1) * 128], xa[kg][:, kk * 128:(kk + 1) * 128], ident) for kk in range(4)])
                for kk in range(4):
                    K.copy("act" if kk % 2 == 0 else "dve", xs[4 * kg + kk][:, tb * 128:(tb + 1) * 128], pt[:, kk * 128:(kk + 1) * 128])
                psp.free(pt)
            f32p.free(*xa)

    def store_x_tm(dst, r0, n):
        for tb in range(n // 128):
            for kg in range(2):
                pt = psp.alloc()
                K.transposes([(pt[:, kk * 128:(kk + 1) * 128], xs[4 * kg + kk][:, tb * 128:(tb + 1) * 128], ident) for kk in range(4)])
                st = f32p.alloc()
                K.copy("act" if kg == 0 else "dve", st[:, :], pt[:, :])
                psp.free(pt)
                K.dma(dst[r0 + tb * 128:r0 + (tb + 1) * 128, kg * 512:(kg + 1) * 512], st[:, :])
                f32p.free(st)

    def layer_setup(l):
        for h in range(4):
            st = f32p.alloc()
            K.dma(st[:, 0:256].rearrange("p (cc n) -> p cc n", cc=2), V(w_uk, w_uk.t[l, :, h, :].rearrange("(cc p) n -> p cc n", p=128)))
            pt = psp.alloc()
            K.transposes([(pt[:, cc * 128:(cc + 1) * 128], st[:, cc * 128:(cc + 1) * 128], ident) for cc in range(2)])
            K.copy("act", wukT[:, h, :], pt[:, 0:256])
            psp.free(pt)
            f32p.free(st)
        for cc in range(2):
            st = f32p.alloc()
            K.dma(st[:, 0:512], V(w_uv, w_uv.t[l, cc * 128:(cc + 1) * 128, :, :].rearrange("p h v -> p (h v)")))
            K.copy("pool", wuv[:, cc, :], st[:, 0:512])
            f32p.free(st)
        for g in range(4):
            st = f32p.alloc()
            K.dma(st[:, 0:128], cws[l, g, :, :])
            K.tt("pool", st[:, 0:128], st[:, 0:128], tril, ALU.mult)
            pt = psp.alloc()
            K.transposes([(pt[:, 0:128], st[:, 0:128], ident)])
            K.copy("act", wTm[:, g, :], pt[:, 0:128])
            psp.free(pt)
            f32p.free(st)
            K.dma(biasb[:, g, :], V(cbs, cbs.t[l, g:g + 1, :].broadcast_to([64, 128])))
        K.dma(vgn[:, :], V(cvg, cvg.t[l:l + 1, :].broadcast_to([128, 256])))

    def global_setup():
        import os
        lvl = int(os.environ.get("GS", "9"))
        K.dma(cst[:, :], cstd[:, :])
        if lvl < 1:
            return
        K.memset("dve", onesb[:, :], 1.0)
        K.memset("dve", zerob[:, :], 0.0)
        K.memset("dve", epsc[:, :], EPS)
        K.copy("dve", identb[:, :], cst[:, 0:128])
        if lvl < 2:
            return
        K.memset("pool", kT[64:65, 2, :], 1.0)
        if lvl < 3:
            return
        st = f32p.alloc()
        K.memset("dve", st[:, 0:128], 0.0)
        K.dma(st[0:96, 0:128], V(norm_g, norm_g.t.rearrange("l i (k p) -> (l i k) p", p=128)))
        K.dma(st[96:102, 0:128], V(qa_g, qa_g.t.rearrange("l (k p) -> (l k) p", p=128)))
        K.dma(st[102:106, 0:128], V(kva_g, kva_g.t.rearrange("l (k p) -> (l k) p", p=128)))
        K.dma(st[106:114, 0:64], V(hog, hog.t.rearrange("l (h p) -> (l h) p", p=64)))
        K.dma(st[114:122, 0:64], V(lb_logits, lb_logits.t.rearrange("l (h p) -> (l h) p", p=64)))
        if lvl < 4:
            f32p.free(st)
            return
        pt = psp.alloc()
        K.transposes([(pt[:, 0:128], st[:, 0:128], ident)])
        K.copy("act", colv[:, :], pt[:, 0:128])
        psp.free(pt)
        K.ts("dve", colvh[:, :], colv[:, :], 0.5, None, ALU.mult)
        f32p.free(st)
        if lvl < 5:
            return
        K.memset("dve", lbc[:, 0:4], 0.0)
        K.memset("dve", lbc[:, 8:12], 1.0)
        d = f32p.alloc()
        K.tt("dve", d[0:64, 0:4], colv[0:64, R_LB + 4:R_LB + 8], colv[0:64, R_LB:R_LB + 4], ALU.subtract)
        K.act(lbc[:, 4:8], d[0:64, 0:4], AF.Sigmoid)
        K.act(lbc[:, 12:16], d[0:64, 0:4], AF.Sigmoid, scale=-1.0)
        f32p.free(d)

    cfg = build_program.cfg

    def drain(g):
        while True:
            try:
                next(g)
            except StopIteration as e:
                return e.value

    def run_interleaved(g1, n1, g2, n2):
        i1 = i2 = 0
        d1 = d2 = False
        v1 = v2 = None
        while not (d1 and d2):
            if (not d1) and (d2 or i1 * n2 <= i2 * n1):
                try:
                    next(g1)
                    i1 += 1
                except StopIteration as e:
                    d1 = True
                    v1 = e.value
            else:
                try:
                    next(g2)
                    i2 += 1
                except StopIteration as e:
                    d2 = True
                    v2 = e.value
        return v1, v2

    def emit():
        global_setup()
        for l in range(cfg["layers"]):
            if cfg.get("lsetup", True):
                layer_setup(l)
            for ti in cfg["tiles"]:
                sample = ti == NPT
                n = SN if sample else TN
                if cfg.get("noload"):
                    pass
                elif l == 0:
                    load_x_tm(xsm if sample else xp, 0 if sample else ti * TN, n)
                else:
                    for k in range(8):
                        K.dma(xs[k][:, :n], xscr[ti][:, k, :n])
                if "ffn1" in cfg["stages"]:
                    ffn(l, 0, n)
                if "mix" in cfg["stages"]:
                    prenorm(l, 2, n)
                    cst_ = cmlp_a(l, ti, n, sample)
                    st = hgrn_prep(l, ti, n, sample)
                    cmlp_b(cst_)
                    if sample:
                        drain(hgrn_elem_gen(st))
                        hgrn_kdT(st)
                        drain(hgrn_chain_gen(st))
                        mixb = hgrn_fin(st)
                        mixc = cmlp_c(cst_)
                        mixa, agen, nblk = drain(mla(l, ti, n, sample))
                    else:
                        (mixa, agen, nblk), _ = run_interleaved(mla(l, ti, n, sample), 12, hgrn_elem_gen(st), 8)
                        mixc = cmlp_c(cst_)
                        hgrn_kdT(st)
                        run_interleaved(agen, nblk, hgrn_chain_gen(st), st["nch"])
                        mixb = hgrn_fin(st)
                    wout(l, n, mixa, mixb, mixc)
                if "ffn2" in cfg["stages"]:
                    ffn(l, 1, n)
                if cfg.get("nostore"):
                    pass
                elif l == cfg["layers"] - 1:
                    store_x_tm(ysm if sample else yp, 0 if sample else ti * TN, n)
                else:
                    for k in range(8):
                        K.dma(xscr[ti][:, k, :n], xs[k][:, :n])
        K.finish()

    K.dry = True
    emit()
    K.dry = False
    ws.reset()
    f32p.free_list = list(f32p.free_list)
    emit()
    build_program.stats = dict(cnt=dict(K.cnt), f32_low=f32p.low, b16_low=b16p.low, ps_low=psp.low, nws=len(ws.specs))

    with nc.Block() as blk:
        @blk.tensor
        def _(h):
            for f in K.prog["pe"]:
                f(h)

        @blk.scalar
        def _(h):
            for f in K.prog["act"]:
                f(h)

        @blk.vector
        def _(h):
            for f in K.prog["dve"]:
                f(h)

        @blk.gpsimd
        def _(h):
            for f in K.prog["pool"]:
                f(h)

        @blk.sync
        def _(h):
            for f in K.prog["sp"]:
                f(h)
    return nc

build_program.cfg = dict(layers=2, tiles=list(range(NPT + 1)), stages=("ffn1", "mix", "ffn2"))

def host_consts():
    cst = np.zeros((128, 1024), np.float32)
    cst[:, 0:128] = np.eye(128, dtype=np.float32)
    cst[:, 128:256] = np.tril(np.ones((128, 128), np.float32))
    tri = np.triu(np.ones((64, 64), np.float32))
    cst[0:64, 256:512] = np.tile(tri, (1, 4))
    rm = np.ones((128, 512), np.float32)
    rm[:, 0::64] = 0.0
    cst[:, 512:1024] = rm
    pos = np.concatenate([np.arange(SEQ), np.tile(PAST + np.arange(64), NSEQ)]).astype(np.float32)
    inv = (10000.0 ** (-np.arange(0, 64, 2, dtype=np.float32) / 64)).astype(np.float32)
    ang = pos[None, :] * inv[:, None]
    rope = np.zeros((64, 2, SEQ + SN), np.float32)
    rope[0:32, 0] = np.cos(ang)
    rope[32:64, 0] = np.cos(ang)
    rope[0:32, 1] = np.sin(ang)
    rope[32:64, 1] = np.sin(ang)
    return cst, rope


def kernel(x_prompt, x_sample, cache_mla_ckv, cache_mla_kpe, state_hgrn, norm_g, ffn_w_gate, ffn_w_up, ffn_w_down,
           w_in, w_out, mla_qa_g, mla_wqb, mla_kva_g, mla_w_uk, mla_w_uv, hgrn_lb_logits, hgrn_out_g, cmlp_v_g,
           cmlp_w_s, cmlp_b_s):
    f = lambda a: np.ascontiguousarray(np.asarray(a, dtype=np.float32))
    nc = build_program()
    cst, rope = host_consts()
    shared = dict(norm_g=f(norm_g), ffn_w_gate=f(ffn_w_gate), ffn_w_up=f(ffn_w_up), ffn_w_down=f(ffn_w_down),
                  w_in=f(w_in), w_out=f(w_out), mla_qa_g=f(mla_qa_g), mla_wqb=f(mla_wqb), mla_kva_g=f(mla_kva_g),
                  mla_w_uk=f(mla_w_uk), mla_w_uv=f(mla_w_uv), hgrn_lb_logits=f(hgrn_lb_logits), hgrn_out_g=f(hgrn_out_g),
                  cmlp_v_g=f(cmlp_v_g), cmlp_w_s=f(cmlp_w_s), cmlp_b_s=f(cmlp_b_s), cst=cst, rope=rope)
    x_prompt = f(x_prompt)
    x_sample = f(x_sample)
    cache_mla_ckv = f(cache_mla_ckv)
    cache_mla_kpe = f(cache_mla_kpe)
    state_hgrn = f(state_hgrn)
    in_maps = []
    for c in range(8):
        b = c % 4
        s0 = c * NSEQ
        m = dict(shared)
        m["xp"] = x_prompt[b]
        m["xsm"] = x_sample[s0:s0 + NSEQ].reshape(SN, D)
        m["ckv"] = np.ascontiguousarray(cache_mla_ckv[:, s0:s0 + NSEQ])
        m["ckpe"] = np.ascontiguousarray(cache_mla_kpe[:, s0:s0 + NSEQ])
        m["shg"] = np.ascontiguousarray(state_hgrn[:, s0:s0 + NSEQ])
        in_maps.append(m)
    res = run_bass_kernel_spmd(nc, in_maps, core_ids=list(range(8)))
    R = res.results
    y_prompt = np.stack([R[b]["yp"] for b in range(4)], 0)
    y_sample = np.concatenate([R[c]["ysm"].reshape(NSEQ, 64, D) for c in range(8)], 0)
    ckv_p = np.stack([R[b]["ckv_p"] for b in range(4)], 1)
    kpe_p = np.stack([R[b]["kpe_p"] for b in range(4)], 1)
    hg_p = np.stack([R[b]["hg_p"] for b in range(4)], 1)
    ckv_s = np.concatenate([R[c]["ckv_s"].reshape(DEPTH, NSEQ, 64, 256) for c in range(8)], 1)
    kpe_s = np.concatenate([R[c]["kpe_s"].reshape(DEPTH, NSEQ, 64, 64) for c in range(8)], 1)
    hg_s = np.concatenate([R[c]["hg_s"] for c in range(8)], 1)
    cv_s = np.concatenate([R[c]["cv_s"].reshape(DEPTH, NSEQ, 64, 256) for c in range(8)], 1)
    return (y_prompt, y_sample, ckv_p, kpe_p, hg_p, ckv_s, kpe_s, hg_s, cv_s)
```
